# Optimizing a Trainium2 kernel written in Bass

```python
import jax, jax.numpy as jnp
from jax import lax
import numpy as np

D_MODEL = 2048
BATCH = 4
SEQ = 2048
DEPTH = 2
DEC_BATCH = 128
DEC_SEQ = 8
PAST_LEN = 16384
PAGE_SIZE = 128

N_MIXERS = 2
N_A = (DEPTH + 1) // 2
N_B = DEPTH // 2
HG_HEADS = 16
HG_DK = 128
HG_DV = D_MODEL // HG_HEADS
HG_FDIM = HG_HEADS * HG_DK
HG_CHUNK = 16
POOL_WINDOWS = (2, 4, 8, 16)
POOL_GROUPS = len(POOL_WINDOWS)
POOL_GC = D_MODEL // POOL_GROUPS
POOL_BUF = max(POOL_WINDOWS) - 1
N_MEM = 256
MEM_HEADS = 4
MEM_HD = D_MODEL // MEM_HEADS
D_FF = 4 * D_MODEL
EPS = 1e-6

kernel_name = "hgrn2_pool_hybrid_decode_step"


def rms_norm(x, w):
    xf = x.astype(jnp.float32)
    y = xf * lax.rsqrt(jnp.mean(xf * xf, axis=-1, keepdims=True) + EPS)
    return (y * w.astype(jnp.float32)).astype(x.dtype)


def lower_bounds(lb_logits):
    sm = jax.nn.softmax(lb_logits.astype(jnp.float32), axis=0)
    return jnp.cumsum(sm, axis=0)


def chunk_gla(q, k, v, logf, S0):
    B, L, H, DK = q.shape
    DV = v.shape[-1]
    C = HG_CHUNK if L % HG_CHUNK == 0 else L
    n = L // C

    def blocks(a):
        return a.reshape(B, n, C, H, a.shape[-1]).swapaxes(0, 1)

    causal = jnp.tril(jnp.ones((C, C), dtype=bool))

    def step(S, inp):
        qc, kc, vc, gc = inp
        b = jnp.cumsum(gc, axis=1)
        b_last = b[:, -1]
        q_dec = qc * jnp.exp(b)
        k_inv = kc * jnp.exp(-b)
        A = jnp.einsum('bthk,bshk->bhts', q_dec, k_inv)
        A = jnp.where(causal, A, 0.0)
        o = jnp.einsum('bhts,bshv->bthv', A, vc) + jnp.einsum('bthk,bhkv->bthv', q_dec, S)
        k_end = kc * jnp.exp(b_last[:, None] - b)
        S = S * jnp.exp(b_last)[..., None] + jnp.einsum('bshk,bshv->bhkv', k_end, vc)
        return S, o

    S, o = lax.scan(step, S0, (blocks(q), blocks(k), blocks(v), blocks(logf)))
    return o.swapaxes(0, 1).reshape(B, L, H, DV), S


def hgrn2_mixer(h, S0, w_in, lb, norm_w, w_out):
    B, L, _ = h.shape
    proj = h @ w_in
    q = proj[..., :HG_FDIM]
    fl = proj[..., HG_FDIM:2 * HG_FDIM]
    i = proj[..., 2 * HG_FDIM:2 * HG_FDIM + D_MODEL]
    g = proj[..., 2 * HG_FDIM + D_MODEL:]
    q = jax.nn.silu(q.astype(jnp.float32)).reshape(B, L, HG_HEADS, HG_DK)
    f = lb + (1.0 - lb) * jax.nn.sigmoid(fl.astype(jnp.float32))
    f = f.reshape(B, L, HG_HEADS, HG_DK)
    k = 1.0 - f
    v = i.astype(jnp.float32).reshape(B, L, HG_HEADS, HG_DV)
    o, S = chunk_gla(q, k, v, jnp.log(f), S0.astype(jnp.float32))
    o = o * lax.rsqrt(jnp.mean(o * o, axis=-1, keepdims=True) + EPS) * norm_w.astype(jnp.float32)
    o = o * jax.nn.silu(g.astype(jnp.float32).reshape(B, L, HG_HEADS, HG_DV))
    return o.reshape(B, L, D_MODEL).astype(h.dtype) @ w_out, S


def pool_mixer(h, buf, pos0, w_in, pool_w, pool_scale, w_out):
    B, L, _ = h.shape
    u = h @ w_in
    ext = jnp.concatenate([buf.astype(u.dtype), u], axis=1)
    cs = jnp.cumsum(ext.astype(jnp.float32), axis=1)
    cs = jnp.concatenate([jnp.zeros((B, 1, D_MODEL), jnp.float32), cs], axis=1)
    pos = pos0 + jnp.arange(L)
    end = cs[:, POOL_BUF + 1:POOL_BUF + 1 + L]
    means = []
    for gi, w in enumerate(POOL_WINDOWS):
        sl = slice(gi * POOL_GC, (gi + 1) * POOL_GC)
        start = cs[:, POOL_BUF + 1 - w:POOL_BUF + 1 - w + L, sl]
        cnt = jnp.minimum(w, pos + 1).astype(jnp.float32)
        means.append((end[..., sl] - start) / cnt[None, :, None])
    pooled = jnp.concatenate(means, axis=-1) - u.astype(jnp.float32)
    pooled = pooled.reshape(B, L, POOL_GROUPS, POOL_GC)
    mixed = jnp.einsum('blgc,gcd->blgd', pooled, pool_w.astype(jnp.float32)).reshape(B, L, D_MODEL)
    mixed = (mixed * pool_scale.astype(jnp.float32)).astype(h.dtype)
    return mixed @ w_out, ext[:, -POOL_BUF:]


def mem_kv(mem, norm_mem, w_xkv):
    B = mem.shape[0]
    kv = rms_norm(mem, norm_mem) @ w_xkv
    k = kv[..., :D_MODEL].reshape(B, N_MEM, MEM_HEADS, MEM_HD)
    v = kv[..., D_MODEL:].reshape(B, N_MEM, MEM_HEADS, MEM_HD)
    return k, v


def mem_cross_attn(h, k, v, w_xq, w_xo):
    B, L, _ = h.shape
    q = (h @ w_xq).reshape(B, L, MEM_HEADS, MEM_HD).astype(jnp.float32)
    s = jnp.einsum('blhd,bmhd->bhlm', q, k.astype(jnp.float32)) * (MEM_HD ** -0.5)
    p = jax.nn.softmax(s, axis=-1)
    o = jnp.einsum('bhlm,bmhd->blhd', p, v.astype(jnp.float32))
    return o.reshape(B, L, D_MODEL).astype(h.dtype) @ w_xo


def trunk(x, pos0, S_in, buf_in, mk, mv, p):
    lbs = lower_bounds(p['hg_lb_logits'])
    S_out, buf_out = [], []
    for l in range(DEPTH):
        j = l // N_MIXERS
        hn = rms_norm(x, p['norm_mix_pre'][l])
        if l % N_MIXERS == 0:
            m, S = hgrn2_mixer(hn, S_in[j], p['w_in_a'][j], lbs[l], p['hg_norm'][j], p['w_out_a'][j])
            S_out.append(S)
        else:
            m, b = pool_mixer(hn, buf_in[j], pos0, p['w_in_b'][j], p['pool_w'][j],
                              p['pool_scale'][j], p['w_out_b'][j])
            buf_out.append(b)
        x = x + rms_norm(m, p['norm_mix_post'][l])
        hn = rms_norm(x, p['norm_x_pre'][l])
        a = mem_cross_attn(hn, mk[l], mv[l], p['w_xq'][l], p['w_xo'][l])
        x = x + rms_norm(a, p['norm_x_post'][l])
        hn = rms_norm(x, p['norm_mlp_pre'][l])
        f = jnp.square(jax.nn.relu(hn @ p['w_up'][l])) @ p['w_down'][l]
        x = x + rms_norm(f, p['norm_mlp_post'][l])
    return x, jnp.stack(S_out), jnp.stack(buf_out)


def setup_inputs(seed: int = 0) -> dict:
    key = jax.random.key(seed)
    ks = jax.random.split(key, 32)
    f32 = jnp.float32

    def nrm(k, shape, scale):
        return scale * jax.random.normal(k, shape, f32)

    def gain(k, shape):
        return 1.0 + 0.05 * jax.random.normal(k, shape, f32)

    sD = D_MODEL ** -0.5
    return {
        "x_prompt": nrm(ks[0], (BATCH, SEQ, D_MODEL), 1.0),
        "x_sample": nrm(ks[1], (DEC_BATCH, DEC_SEQ, D_MODEL), 1.0),
        "state_hgrn": nrm(ks[2], (N_A, DEC_BATCH, HG_HEADS, HG_DK, HG_DV), 0.5),
        "state_pool": nrm(ks[3], (N_B, DEC_BATCH, POOL_BUF, D_MODEL), 1.0),
        "cache_mem_k": nrm(ks[4], (DEPTH, DEC_BATCH, N_MEM, MEM_HEADS, MEM_HD), 1.0),
        "cache_mem_v": nrm(ks[5], (DEPTH, DEC_BATCH, N_MEM, MEM_HEADS, MEM_HD), 1.0),
        "mem_prompt": nrm(ks[6], (BATCH, N_MEM, D_MODEL), 1.0),
        "w_in_a": nrm(ks[7], (N_A, D_MODEL, 2 * HG_FDIM + 2 * D_MODEL), sD),
        "hg_lb_logits": nrm(ks[8], (DEPTH + 1, HG_FDIM), 0.1),
        "hg_norm": gain(ks[9], (N_A, HG_DV)),
        "w_out_a": nrm(ks[10], (N_A, D_MODEL, D_MODEL), sD),
        "w_in_b": nrm(ks[11], (N_B, D_MODEL, D_MODEL), sD),
        "pool_w": nrm(ks[12], (N_B, POOL_GROUPS, POOL_GC, POOL_GC), POOL_GC ** -0.5),
        "pool_scale": 1.0 + 0.1 * jax.random.normal(ks[13], (N_B, D_MODEL), f32),
        "w_out_b": nrm(ks[14], (N_B, D_MODEL, D_MODEL), sD),
        "norm_mem": gain(ks[15], (DEPTH, D_MODEL)),
        "w_xq": nrm(ks[16], (DEPTH, D_MODEL, D_MODEL), sD),
        "w_xkv": nrm(ks[17], (DEPTH, D_MODEL, 2 * D_MODEL), sD),
        "w_xo": nrm(ks[18], (DEPTH, D_MODEL, D_MODEL), sD),
        "norm_mix_pre": gain(ks[19], (DEPTH, D_MODEL)),
        "norm_mix_post": gain(ks[20], (DEPTH, D_MODEL)),
        "norm_x_pre": gain(ks[21], (DEPTH, D_MODEL)),
        "norm_x_post": gain(ks[22], (DEPTH, D_MODEL)),
        "norm_mlp_pre": gain(ks[23], (DEPTH, D_MODEL)),
        "norm_mlp_post": gain(ks[24], (DEPTH, D_MODEL)),
        "w_up": nrm(ks[25], (DEPTH, D_MODEL, D_FF), sD),
        "w_down": nrm(ks[26], (DEPTH, D_FF, D_MODEL), D_FF ** -0.5),
    }


def reference(x_prompt, x_sample, state_hgrn, state_pool, cache_mem_k, cache_mem_v, mem_prompt,
              w_in_a, hg_lb_logits, hg_norm, w_out_a, w_in_b, pool_w, pool_scale, w_out_b,
              norm_mem, w_xq, w_xkv, w_xo, norm_mix_pre, norm_mix_post, norm_x_pre, norm_x_post,
              norm_mlp_pre, norm_mlp_post, w_up, w_down):
    p = dict(w_in_a=w_in_a, hg_lb_logits=hg_lb_logits, hg_norm=hg_norm, w_out_a=w_out_a,
             w_in_b=w_in_b, pool_w=pool_w, pool_scale=pool_scale, w_out_b=w_out_b,
             w_xq=w_xq, w_xo=w_xo, norm_mix_pre=norm_mix_pre, norm_mix_post=norm_mix_post,
             norm_x_pre=norm_x_pre, norm_x_post=norm_x_post, norm_mlp_pre=norm_mlp_pre,
             norm_mlp_post=norm_mlp_post, w_up=w_up, w_down=w_down)
    mk_list, mv_list = [], []
    for l in range(DEPTH):
        k, v = mem_kv(mem_prompt, norm_mem[l], w_xkv[l])
        mk_list.append(k)
        mv_list.append(v)
    cache_mem_k_prompt = jnp.stack(mk_list)
    cache_mem_v_prompt = jnp.stack(mv_list)
    S0 = jnp.zeros((N_A, BATCH, HG_HEADS, HG_DK, HG_DV), jnp.float32)
    buf0 = jnp.zeros((N_B, BATCH, POOL_BUF, D_MODEL), x_prompt.dtype)
    y_prompt, state_hgrn_prompt, state_pool_prompt = trunk(
        x_prompt, 0, S0, buf0, cache_mem_k_prompt, cache_mem_v_prompt, p)
    y_sample, state_hgrn_sample, state_pool_sample = trunk(
        x_sample, PAST_LEN, state_hgrn, state_pool, cache_mem_k, cache_mem_v, p)
    return (y_prompt, y_sample, state_hgrn_prompt, state_pool_prompt, cache_mem_k_prompt,
            cache_mem_v_prompt, state_hgrn_sample, state_pool_sample)
```

```python
import os
import numpy as np
import concourse.bass as bass
import concourse.mybir as mybir
from concourse.bass_utils import run_bass_kernel_spmd

F32 = mybir.dt.float32
BF16 = mybir.dt.bfloat16
AF = mybir.ActivationFunctionType
ALU = mybir.AluOpType

D = 2048
NCH = 16
T = 608
EPS = 1e-6
PASSES = [dict(nch=9, seqs=list(range(0, 4)), row0=0), dict(nch=8, seqs=list(range(4, 16)), row0=576)]
NPRE = 960
NBLK = 256
TPRE = 320
V_MIXPRE, V_MIXPOST, V_XPRE, V_XPOST, V_MLPPRE, V_MLPPOST, V_MEM, V_PSCALE = 0, 2, 4, 6, 8, 10, 12, 14


class Sync:
    def __init__(self, nc, ndma_sems=12):
        self.nc = nc
        self.eng = {"pe": nc.tensor, "act": nc.scalar, "dve": nc.vector, "pool": nc.gpsimd, "sp": nc.sync}
        self._cms = []
        self.sem = {}
        for e in self.eng:
            cm = nc.semaphore("sem_" + e)
            self.sem[e] = cm.__enter__()
            self._cms.append(cm)
        self.cnt = {e: 0 for e in self.eng}
        self.known = {e: {} for e in self.eng}
        self.dsem = {}
        self.dval = {}
        self.drot = {}
        for q in ("sp", "pool", "act"):
            lst = []
            for i in range(ndma_sems):
                cm = nc.semaphore("dsem_%s_%d" % (q, i))
                lst.append(cm.__enter__())
                self._cms.append(cm)
            self.dsem[q] = lst
            self.dval[q] = [0] * ndma_sems
            self.drot[q] = 0
        self.last_w = {}
        self.readers = {}
        self.ninstr = 0

    def close(self):
        for cm in reversed(self._cms):
            cm.__exit__(None, None, None)

    def _wait(self, e, tok):
        if tok[0] == "e":
            _, pe, n = tok
            if pe == e and e == "pe":
                return
            key = pe
            semh = self.sem[pe]
            val = n
        else:
            _, q, idx, val = tok
            key = (q, idx)
            semh = self.dsem[q][idx]
        if self.known[e].get(key, 0) >= val:
            return
        self.eng[e].wait_ge(semh, val)
        self.known[e][key] = val

    def _deps(self, e, reads, writes):
        toks = []
        for r in reads:
            t = self.last_w.get(r)
            if t is not None:
                toks.append(t)
        for w in writes:
            t = self.last_w.get(w)
            if t is not None:
                toks.append(t)
            toks.extend(self.readers.get(w, ()))
        for t in toks:
            self._wait(e, t)

    def _record(self, tok, reads, writes):
        for r in reads:
            lst = self.readers.setdefault(r, [])
            if tok not in lst:
                lst.append(tok)
        for w in writes:
            self.last_w[w] = tok
            self.readers[w] = []

    def op(self, e, fn, reads=(), writes=(), signal=True):
        self._deps(e, reads, writes)
        ins = fn(self.eng[e])
        self.ninstr += 1
        if signal:
            self.cnt[e] += 1
            ins.then_inc(self.sem[e], 1)
            tok = ("e", e, self.cnt[e])
        else:
            tok = ("e", e, self.cnt[e] + 1)
        self._record(tok, reads, writes)
        return ins

    def dma(self, q, out, in_, reads=(), writes=(), **kw):
        self._deps(q, reads, writes)
        idx = self.drot[q]
        self.drot[q] = (idx + 1) % len(self.dsem[q])
        if self.dval[q][idx] > 0:
            self._wait(q, ("d", q, idx, self.dval[q][idx]))
        ins = self.eng[q].dma_start(out=out, in_=in_, **kw)
        self.ninstr += 1
        self.dval[q][idx] += 16
        ins.then_inc(self.dsem[q][idx], 16)
        tok = ("d", q, idx, self.dval[q][idx])
        self._record(tok, reads, writes)
        return tok

    def barrier(self):
        for e in self.eng:
            for q in self.dsem:
                for idx, v in enumerate(self.dval[q]):
                    if v > 0:
                        self._wait(e, ("d", q, idx, v))
            for pe in self.eng:
                if pe != e and self.cnt[pe] > 0:
                    self._wait(e, ("e", pe, self.cnt[pe]))
        self.last_w = {}
        self.readers = {}


class _Stop(Exception):
    pass


def build_program(stop=None):
    nc = bass.Bass("TRN2", target_bir_lowering=False)
    live = []

    def checkpoint(name):
        if stop == name:
            raise _Stop()

    def din(name, shape):
        return nc.dram_tensor(name, list(shape), F32, kind="ExternalInput").ap()

    def dout(name, shape):
        return nc.dram_tensor(name, list(shape), F32, kind="ExternalOutput").ap()

    xp = din("xp", [1088, D])
    xpre = din("xpre", [NPRE, D])
    xs = din("xs", [128, D])
    sh = din("sh", [16, 16, 128, 128])
    spool = din("spool", [16, 15, D])
    ck = din("ck", [2, 16, 256, D])
    cv = din("cv", [2, 16, 256, D])
    mem = din("mem", [256, D])
    wblk = din("wblk", [NBLK, 128, 4096])
    vec_d = din("vec", [128, 15, 16])
    lbl_d = din("lbl", [128, 3, 16])
    hgn_d = din("hgn", [128, 1])
    flag_d = din("flag", [128, 1])
    ident_d = din("ident", [128, 128])
    mc_d = din("mc", [64, 64])
    ms_d = din("ms", [32, 32])
    seqm_d = din("seqm", [32, 4])
    rm_d = din("rm", [2, 128, T])
    rmpre_d = din("rmpre", [128, TPRE])
    invc_d = din("invc", [2, 4, 128, T])

    y_p = dout("y_p", [1024, D])
    y_s = dout("y_s", [128, D])
    S_p = dout("S_p", [16, 128, 128])
    pool_p = dout("pool_p", [15, D])
    mk_o = dout("mk_o", [2, 256, D])
    mv_o = dout("mv_o", [2, 256, D])
    S_s = dout("S_s", [16, 16, 128, 128])
    pool_s = dout("pool_s", [16, 15, D])

    S = Sync(nc)
    cms = []

    def sb(name, shape, dt=F32):
        cm = nc.sbuf_tensor("s_" + name, list(shape), dt)
        cms.append(cm)
        return cm.__enter__()

    def psum(name, shape, dt=F32):
        cm = nc.psum_tensor("p_" + name, list(shape), dt)
        cms.append(cm)
        return cm.__enter__()

    class Local:
        def __init__(self):
            self.l = []
            live.append(self)

        def sb(self, name, shape, dt=F32):
            st["uid"] = st.get("uid", 0) + 1
            cm = nc.sbuf_tensor("l%d_%s" % (st["uid"], name), list(shape), dt)
            self.l.append(cm)
            return cm.__enter__()

        def free(self):
            S.barrier()
            for cm in reversed(self.l):
                cm.__exit__(None, None, None)
            self.l = []
            live.remove(self)

    xT = sb("xT", [128, NCH, T])
    hn = sb("hn", [128, NCH, T], BF16)
    Fb = sb("Fb", [128, NCH, T])
    ring = [sb("ring%d" % i, [128, 4096], BF16) for i in range(3)]
    KT = [sb("KT%d" % l, [128, 16, 256], BF16) for l in range(2)]
    VV = [sb("VV%d" % l, [128, 2, D], BF16) for l in range(2)]
    Sst = sb("Sst", [128, 16, 128])
    vec = sb("vec", [128, 15, 16])
    lbl = sb("lbl", [128, 3, 16])
    lbv = sb("lbv", [128, 16])
    oml = sb("oml", [128, 16])
    hgn = sb("hgn", [128, 1])
    flag = sb("flag", [128, 1])
    identf = sb("identf", [128, 128])
    identb = sb("identb", [128, 128], BF16)
    onesb = sb("onesb", [128, 128], BF16)
    epst = sb("epst", [128, 1])
    mc = sb("mc", [64, 64])
    ms = sb("ms", [32, 32])
    seqm = sb("seqm", [32, 4])
    rm = sb("rm", [128, T])
    rstd = sb("rstd", [128, T])
    sq = [sb("sq%d" % i, [128, T], BF16) for i in range(2)]
    tailbuf = sb("tailbuf", [128, NCH, 15])

    PA = psum("PA", [128, 2, 512])
    PB = psum("PB", [128, 2, 512])
    PN = psum("PN", [128, 2, 512])
    PG = psum("PG", [128, 512])
    PT = psum("PT", [128, 1024], BF16)
    st = dict(ring=0, px=0, sq=0)

    def act(out, in_, func, reads, writes, **kw):
        S.op("act", lambda e: e.activation(out=out, in_=in_, func=func, **kw), reads, writes)

    def tt(out, in0, in1, op, reads, writes):
        S.op("dve", lambda e: e.tensor_tensor(out=out, in0=in0, in1=in1, op=op), reads, writes)

    def tsc(out, in0, s1, s2, op0, op1, reads, writes):
        S.op("dve", lambda e: e.tensor_scalar(out=out, in0=in0, scalar1=s1, scalar2=s2, op0=op0, op1=op1), reads, writes)

    def stt(out, in0, scalar, in1, op0, op1, reads, writes):
        S.op("dve", lambda e: e.scalar_tensor_tensor(out=out, in0=in0, scalar=scalar, in1=in1, op0=op0, op1=op1),
             reads, writes)

    def mm(out, lhsT, rhs, start, stop, reads, writes, signal):
        S.op("pe", lambda e: e.matmul(out, lhsT=lhsT, rhs=rhs, start=start, stop=stop, skip_group_check=True),
             reads, writes, signal=signal)

    def tr(out, in_, ident, reads, writes, signal=True):
        S.op("pe", lambda e: e.transpose(out=out, in_=in_, identity=ident), reads, writes, signal=signal)

    def halves(ap, tt_):
        return ap.rearrange("p (h n) -> p h n", h=2)

    specs = {}

    def wload(name, idx, r0, nk, c0, C):
        spec = (name, idx, r0, nk, c0, C)
        if spec not in specs:
            specs[spec] = len(specs)
        bid = specs[spec]
        slot = st["ring"] % 3
        st["ring"] += 1
        view = ring[slot][:, 0:nk * C].rearrange("p (k c) -> p k c", c=C)
        S.dma("pool", ring[slot][:, 0:nk * C], wblk[bid, :, 0:nk * C], writes=[("ring", slot)])
        return view, ("ring", slot)

    def next_px():
        st["px"] += 1
        return (PA, "PA") if st["px"] % 2 else (PB, "PB")

    def fm_proj(wv, wkey, nk, nj, src, skey, tcols, evac):
        nh = tcols // 2
        for j in range(nj):
            PX, pkey = next_px()
            for kc in range(nk):
                for h in range(2):
                    mm(PX[:, h, 0:nh], wv[:, kc, j * 128:(j + 1) * 128], src[:, kc, h * nh:(h + 1) * nh],
                       kc == 0, kc == nk - 1, [wkey, (skey, kc)], [pkey], signal=(kc == nk - 1 and h == 1))
            evac(j, PX[:, :, 0:nh], pkey)

    def norm_stats(src, skey, tcols, scale):
        nh = tcols // 2
        for c in range(NCH):
            sqb = sq[st["sq"] % 2]
            sk = ("sq", st["sq"] % 2)
            st["sq"] += 1
            act(sqb[:, 0:tcols], src[:, c, 0:tcols], AF.Square, [(skey, c)], [sk])
            for h in range(2):
                mm(PN[:, h, 0:nh], onesb[:], sqb[:, h * nh:(h + 1) * nh], c == 0, c == NCH - 1, [sk, "onesb"], ["PN"],
                   signal=(h == 1))
        r3 = rstd[:, 0:tcols].rearrange("p (h n) -> p h n", h=2)
        act(r3, PN[:, :, 0:nh], AF.Ln, ["PN", "epst"], ["rstd"], scale=scale, bias=epst[:, 0:1])
        act(rstd[:, 0:tcols], rstd[:, 0:tcols], AF.Exp, ["rstd"], ["rstd"], scale=-0.5)

    def prenorm(vidx, tcols):
        norm_stats(xT, "x", tcols, 1.0 / D)
        for c in range(NCH):
            stt(hn[:, c, 0:tcols], xT[:, c, 0:tcols], vec[:, vidx, c:c + 1], rstd[:, 0:tcols], ALU.mult, ALU.mult,
                [("x", c), "rstd", "vec"], [("hn", c)])

    def postnorm(vidx):
        norm_stats(Fb, "F", T, 1.0 / D)
        for c in range(NCH):
            stt(Fb[:, c, :], Fb[:, c, :], vec[:, vidx, c:c + 1], rstd[:, :], ALU.mult, ALU.mult,
                [("F", c), "rstd", "vec"], [("F", c)])
            tt(xT[:, c, :], xT[:, c, :], Fb[:, c, :], ALU.add, [("x", c), ("F", c)], [("x", c)])

    def out_proj(src, skey, nk, wname, widx, wr0, first):
        C = 4096 // nk
        nj = C // 128
        for b in range(D // C):
            wv, wkey = wload(wname, widx, wr0, nk, b * C, C)

            def evac(j, P3, pkey, b=b):
                oc = b * nj + j
                dst = Fb[:, oc, :].rearrange("p (h n) -> p h n", h=2)
                if first:
                    act(dst, P3, AF.Copy, [pkey], [("F", oc)])
                else:
                    tt(dst, dst, P3, ALU.add, [pkey, ("F", oc)], [("F", oc)])
            fm_proj(wv, wkey, nk, nj, src, skey, T, evac)

    def load_x(rows_ap, nrows, col0, L):
        r = 0
        i = 0
        while r < nrows:
            n = min(128, nrows - r)
            stg = L["stage"][i % 2]
            sk = ("stage", i % 2)
            S.dma("sp", stg[0:n, :], rows_ap[r:r + n, :], writes=[sk])
            for g in range(4):
                PX, pkey = next_px()
                for q in range(4):
                    c = g * 4 + q
                    tr(PX[:, q // 2, (q % 2) * 256:(q % 2) * 256 + n], stg[0:n, c * 128:(c + 1) * 128], identf[0:n, 0:n],
                       [sk, "identf"], [pkey], signal=(q == 3))
                src = PX[:].rearrange("p h (q n) -> p (h q) n", q=2)[:, :, 0:n]
                act(xT[:, g * 4:(g + 1) * 4, col0 + r:col0 + r + n], src, AF.Copy, [pkey],
                    [("x", c_) for c_ in range(g * 4, g * 4 + 4)])
            r += n
            i += 1

    def store_rows(srcT, skey, col0, nrows, dst_ap, L, nchunks=NCH, dcol0=0):
        r = 0
        i = st.get("stg", 0)
        while r < nrows:
            n = min(128, nrows - r)
            stg = L["stage"][i % 2]
            sk = ("stage", i % 2)
            for g in range(nchunks // 4):
                PX, pkey = next_px()
                for q in range(4):
                    c = g * 4 + q
                    tr(PX[0:n, q // 2, (q % 2) * 128:(q % 2) * 128 + 128], srcT[:, c, col0 + r:col0 + r + n], identf[:, :],
                       [(skey, c), "identf"], [pkey], signal=(q == 3))
                src = PX[0:n, :, 0:256]
                dst = stg[0:n, g * 512:(g + 1) * 512].rearrange("p (h n) -> p h n", h=2)
                act(dst, src, AF.Copy, [pkey], [sk])
            S.dma("sp", dst_ap[r:r + n, dcol0:dcol0 + nchunks * 128], stg[0:n, 0:nchunks * 128], reads=[sk])
            r += n
            i += 1
        st["stg"] = i

    def dbg_dump(name, ap, shape, reads):
        if os.environ.get("DBGDUMP"):
            d = nc.dram_tensor(name, list(shape), F32, kind="ExternalOutput").ap()
            S.dma("sp", d, ap, reads=reads)

    S.dma("sp", vec[:], vec_d[:, :, :], writes=["vec"])
    S.dma("sp", lbl[:], lbl_d[:, :, :], writes=["lbl"])
    S.dma("sp", hgn[:], hgn_d[:, :], writes=["hgn"])
    S.dma("sp", flag[:], flag_d[:, :], writes=["flag"])
    S.dma("sp", identf[:], ident_d[:, :], writes=["identf"])
    S.dma("pool", identb[:], ident_d[:, :], writes=["identb"])
    S.dma("sp", mc[:], mc_d[:, :], writes=["mc"])
    S.dma("sp", ms[:], ms_d[:, :], writes=["ms"])
    S.dma("sp", seqm[:], seqm_d[:, :], writes=["seqm"])
    S.op("dve", lambda e: e.memset(onesb[:], 1.0), [], ["onesb"])
    S.op("dve", lambda e: e.memset(epst[:], EPS), [], ["epst"])
    S.op("dve", lambda e: e.memset(Sst[:], 0.0), [], ["Sst"])
    S.op("dve", lambda e: e.memset(tailbuf[:], 0.0), [], ["tail"])
    act(lbl[:], lbl[:], AF.Exp, ["lbl"], ["lbl"])
    tt(oml[:], lbl[:, 0, :], lbl[:, 1, :], ALU.add, ["lbl"], ["oml"])
    tt(oml[:], oml[:], lbl[:, 2, :], ALU.add, ["oml", "lbl"], ["oml"])
    S.op("dve", lambda e: e.reciprocal(out=oml[:], in_=oml[:]), ["oml"], ["oml"])
    tt(lbv[:], lbl[:, 0, :], oml[:], ALU.mult, ["lbl", "oml"], ["lbv"])
    tsc(oml[:], lbv[:], -1.0, 1.0, ALU.mult, ALU.add, ["lbv"], ["oml"])
    S.barrier()

    def body():
        L = Local()
        Ld = dict(stage=[L.sb("stage0", [128, D]), L.sb("stage1", [128, D])])
        memn = L.sb("memn", [128, NCH, 256], BF16)
        checkpoint("const")
        load_x(mem, 256, 0, Ld)
        checkpoint("memload")
        norm_stats(xT, "x", 256, 1.0 / D)
        checkpoint("memnorm")
        for l in range(2):
            for c in range(NCH):
                stt(memn[:, c, :], xT[:, c, 0:256], vec[:, V_MEM + l, c:c + 1], rstd[:, 0:256], ALU.mult, ALU.mult,
                    [("x", c), "rstd", "vec"], [("memn", c)])
            checkpoint("memn%d" % l)
            for kv in range(2):
                outd = mk_o if kv == 0 else mv_o
                for b in range(8):
                    wv, wkey = wload("w_xkv", l, 0, 16, kv * D + b * 256, 256)
                    for mt in range(2):
                        PX, pkey = next_px()
                        for kc in range(NCH):
                            mm(PX[:, 0, 0:256], memn[:, kc, mt * 128:(mt + 1) * 128], wv[:, kc, :], kc == 0, kc == NCH - 1,
                               [wkey, ("memn", kc)], [pkey], signal=(kc == NCH - 1))
                        stg = Ld["stage"][mt]
                        act(stg[:, b * 256:(b + 1) * 256], PX[:, 0, 0:256], AF.Copy, [pkey], [("stage", mt)])
                        if kv == 1:
                            S.op("dve", lambda e, mt=mt, b=b, stg=stg: e.tensor_copy(out=VV[l][:, mt, b * 256:(b + 1) * 256],
                                                                                     in_=stg[:, b * 256:(b + 1) * 256]),
                                 [("stage", mt)], [("VV", l)])
                    checkpoint("tok%d%d%d" % (l, kv, b))
                    if kv == 0:
                        for j in range(2):
                            PX, pkey = next_px()
                            for kc in range(NCH):
                                mm(PX[:, 0, 0:256], wv[:, kc, j * 128:(j + 1) * 128], memn[:, kc, :], kc == 0, kc == NCH - 1,
                                   [wkey, ("memn", kc)], [pkey], signal=(kc == NCH - 1))
                            act(KT[l][:, b * 2 + j, :], PX[:, 0, 0:256], AF.Copy, [pkey], [("KT", l)])
                checkpoint("blk%d%d" % (l, kv))
                for mt in range(2):
                    S.dma("sp", outd[l, mt * 128:(mt + 1) * 128, :], Ld["stage"][mt][:, :], reads=[("stage", mt)])
                checkpoint("out%d%d" % (l, kv))
        L.free()
        checkpoint("memkv")

        def hgrn_group(h0, tcols, nchunks, seqs, Lh, prefix, first):
            Tp = nchunks * 64
            qT, W1, W2, KBb, TMP, iT, gT, onr, EL = (Lh[k] for k in ("qT", "W1", "W2", "KB", "TMP", "iT", "gT", "onr", "EL"))
            ns = len(seqs)

            def ev(dst, func, nm):
                def f(j, P3, pkey):
                    act(dst[:, j, 0:tcols].rearrange("p (h n) -> p h n", h=2), P3, func, [pkey], [(nm, j)])
                return f
            rmask = Lh["rmask"]

            def proj(nm, dst, func, blk):
                wv, wkey = wload("w_in_a", 0, 0, 16, blk * D + h0 * 128, 256)
                fm_proj(wv, wkey, 16, 2, hn, "hn", tcols, ev(dst, func, nm))

            proj("W1", W1, AF.Sigmoid, 1)
            J = range(2)
            w1 = [W1[:, j, 0:tcols] for j in J]
            w2 = [W2[:, j, 0:tcols] for j in J]
            kb = [KBb[:, j, 0:tcols] for j in J]
            tmp = [TMP[:, j, 0:tcols] for j in J]
            k1 = [("W1", j) for j in J]
            k2 = [("W2", j) for j in J]
            kk = [("KB", j) for j in J]
            kt = [("TMP", j) for j in J]
            ke = [("EL", j) for j in J]
            for j in J:
                tsc(w1[j], w1[j], oml[:, h0 + j:h0 + j + 1], lbv[:, h0 + j:h0 + j + 1], ALU.mult, ALU.add, [k1[j], "oml", "lbv"], [k1[j]])
            for j in J:
                tsc(kb[j], w1[j], -1.0, 1.0, ALU.mult, ALU.add, [k1[j]], [kk[j]])
            for j in J:
                act(w1[j], w1[j], AF.Ln, [k1[j]], [k1[j]])
            for j in J:
                S.op("dve", lambda e, j=j: e.tensor_tensor_scan(out=w2[j], data0=rmask[:, 0:tcols], data1=w1[j], initial=0.0,
                                                                op0=ALU.mult, op1=ALU.add), [k1[j], "rm"], [k2[j]])
            for j in J:
                act(tmp[j], w2[j], AF.Exp, [k2[j]], [kt[j]], scale=-1.0)
            for j in J:
                tt(kb[j], kb[j], tmp[j], ALU.mult, [kk[j], kt[j]], [kk[j]])
            for j in J:
                if Tp:
                    act(EL[:, j, 0:nchunks], w2[j][:, 63:Tp:64], AF.Exp, [k2[j]], [ke[j]])
                if ns:
                    act(EL[:, j, nchunks:nchunks + ns], w2[j][:, Tp + 7:tcols:8], AF.Exp, [k2[j]], [ke[j]])
            for j in J:
                if Tp:
                    tt(tmp[j][:, 0:Tp].rearrange("p (c i) -> p c i", i=64), kb[j][:, 0:Tp].rearrange("p (c i) -> p c i", i=64),
                       EL[:, j, 0:nchunks].unsqueeze(2).broadcast_to([128, nchunks, 64]), ALU.mult, [kk[j], ke[j]], [kt[j]])
                if ns:
                    tt(tmp[j][:, Tp:tcols].rearrange("p (c i) -> p c i", i=8), kb[j][:, Tp:tcols].rearrange("p (c i) -> p c i", i=8),
                       EL[:, j, nchunks:nchunks + ns].unsqueeze(2).broadcast_to([128, ns, 8]), ALU.mult, [kk[j], ke[j]], [kt[j]])
            proj("iT", iT, AF.Copy, 2)
            if not prefix:
                proj("qT", qT, AF.Silu, 0)
                for j in J:
                    act(w1[j], w2[j], AF.Exp, [k2[j]], [k1[j]])
                for j in J:
                    tt(qT[:, j, 0:tcols], qT[:, j, 0:tcols], w1[j], ALU.mult, [("qT", j), k1[j]], [("qT", j)])
                proj("gT", gT, AF.Silu, 3)
            if not prefix:
                checkpoint("hg_chain")
            chunks = [(c * 64, 64, "p", None, c) for c in range(nchunks)]
            for s0 in range(0, ns, 4):
                chunks.append((Tp + s0 * 8, 32, "s", seqs[s0:s0 + 4], nchunks + s0))
            NB = 3
            nck = len(chunks)
            SbL = Lh["SbL"]
            if not prefix:
                act(SbL[0][:], Sst[:, h0:h0 + 2, :], AF.Copy, [("Sst", h0), ("Sst", h0 + 1)], [("SbL", 0)])

            def bufs(ci):
                b2 = ci % NB
                return Lh["Vtok"][b2], Lh["Ktok"][b2], ("Vtok", b2), ("Ktok", b2), b2

            def stT(ci):
                c0, n, kind, sq_, eli = chunks[ci]
                Vtok, Ktok, kV, kK, b2 = bufs(ci)
                for j in range(2):
                    tr(PT[0:n, j * 128:(j + 1) * 128], iT[:, j, c0:c0 + n], identb[:, :], [("iT", j), "identb"], ["PT"], signal=False)
                    tr(PT[0:n, 256 + j * 128:256 + (j + 1) * 128], TMP[:, j, c0:c0 + n], identb[:, :], [("TMP", j), "identb"], ["PT"],
                       signal=(j == 1))
                act(Vtok[0:n, :], PT[0:n, 0:256], AF.Copy, ["PT"], [kV])
                act(Ktok[0:n, :], PT[0:n, 256:512], AF.Copy, ["PT"], [kK])

            def stA(ci):
                c0, n, kind, sq_, eli = chunks[ci]
                b2 = ci % NB
                AT = Lh["AT"][b2]
                for j in range(2):
                    mm(PN[0:n, 0, j * 64:j * 64 + n], KBb[:, j, c0:c0 + n], qT[:, j, c0:c0 + n], True, True,
                       [("KB", j), ("qT", j)], ["PN"], signal=(j == 1))
                msk = mc if kind == "p" else ms
                tt(AT[0:n, :, 0:n], PN[0:n, 0, 0:128].rearrange("p (j t) -> p j t", j=2)[:, :, 0:n],
                   msk[0:n, 0:n].unsqueeze(1).broadcast_to([n, 2, n]), ALU.mult, ["PN", "mc", "ms"], [("AT", b2)])

            def stK(ci):
                c0, n, kind, sq_, eli = chunks[ci]
                if kind != "p":
                    return
                Vtok, Ktok, kV, kK, b2 = bufs(ci)
                PK, kPK = (PG[:, 0:256], "PG") if ci % 2 == 0 else (PB[:, 0, 0:256], "PB")
                for j in range(2):
                    mm(PK[:, j * 128:(j + 1) * 128], Ktok[0:n, j * 128:(j + 1) * 128], Vtok[0:n, j * 128:(j + 1) * 128], True, True,
                       [kK, kV], [kPK], signal=(j == 1))
                nxt = (ci + 1) % 3
                for j in range(2):
                    stt(Sst[:, h0 + j, :], Sst[:, h0 + j, :], EL[:, j, eli:eli + 1], PK[:, j * 128:(j + 1) * 128], ALU.mult, ALU.add,
                        [("Sst", h0 + j), ("EL", j), kPK], [("Sst", h0 + j)])
                if not prefix:
                    act(SbL[nxt][:], Sst[:, h0:h0 + 2, :], AF.Copy, [("Sst", h0), ("Sst", h0 + 1)], [("SbL", nxt)])

            def stO(ci):
                c0, n, kind, sq_, eli = chunks[ci]
                Vtok, Ktok, kV, kK, b2 = bufs(ci)
                AT = Lh["AT"][b2]
                PO, kPO = (PN[:, 1, 0:128], "PN1") if ci % 2 == 0 else (PA[:, 0, 0:128], "PA")
                cur = ci % 3
                if kind == "s":
                    S.op("dve", lambda e: e.memset(PO, 0.0), [], [kPO])
                for j in range(2):
                    mm(PO[:, j * 64:j * 64 + n], Vtok[0:n, j * 128:(j + 1) * 128], AT[0:n, j, 0:n], kind == "p", False,
                       [kV, ("AT", b2)], [kPO], signal=False)
                    if kind == "p":
                        mm(PO[:, j * 64:j * 64 + n], SbL[cur][:, j, :], qT[:, j, c0:c0 + n], False, True,
                           [("SbL", cur), ("qT", j)], [kPO], signal=(j == 1))
                if kind == "s":
                    for qi, s in enumerate(sq_):
                        act(Lh["Ssb"][qi][:], Lh["Ssf"][qi][:], AF.Copy, [("Ssf", qi)], [("Ssb", qi)])
                    for qi, s in enumerate(sq_):
                        b3 = qi
                        Ssf, Ssb = Lh["Ssf"][b3], Lh["Ssb"][b3]
                        for j in range(2):
                            mm(PO[:, j * 64 + qi * 8:j * 64 + qi * 8 + 8], Ssb[:, j, :], qT[:, j, c0 + qi * 8:c0 + qi * 8 + 8], False, True,
                               [("Ssb", b3), ("qT", j)], [kPO], signal=(j == 1))
                        Ktm = Lh["Ktm"]
                        S.op("dve", lambda e, Ktm=Ktm, Ktok=Ktok, n=n, qi=qi: e.tensor_scalar(
                            out=Ktm[0:n, :], in0=Ktok[0:n, :], scalar1=seqm[0:n, qi:qi + 1], scalar2=None, op0=ALU.mult),
                            [kK, "seqm"], ["Ktm"])
                        for j in range(2):
                            mm(PG[:, j * 128:(j + 1) * 128], Ktm[0:n, j * 128:(j + 1) * 128], Vtok[0:n, j * 128:(j + 1) * 128], True, True,
                               ["Ktm", kV], ["PG"], signal=(j == 1))
                        So = Lh["Sout"][qi % 2]
                        for j in range(2):
                            stt(So[:, j, :], Ssf[:, j, :], EL[:, j, eli + qi:eli + qi + 1], PG[:, j * 128:(j + 1) * 128], ALU.mult, ALU.add,
                                [("Ssf", b3), ("EL", j), "PG"], [("Sout", qi % 2)])
                        S.dma("sp", S_s[s, h0:h0 + 2, :, :].rearrange("h k v -> k h v"), So[:], reads=[("Sout", qi % 2)])
                        if ci + 1 < nck and qi < len(chunks[ci + 1][3]):
                            load_state(chunks[ci + 1][3][qi], qi)
                o3 = PO.rearrange("p (j t) -> p j t", j=2)[:, :, 0:n]
                act(W1[:, :, c0:c0 + n], o3, AF.Copy, [kPO], [("W1", 0), ("W1", 1)])

            def load_state(s, qi):
                S.dma("sp", Lh["Ssf"][qi][:], sh[s, h0:h0 + 2, :, :].rearrange("h k v -> k h v"), writes=[("Ssf", qi)])

            if not prefix and nck > nchunks:
                for qi, s in enumerate(chunks[nchunks][3]):
                    load_state(s, qi)
            for c_ in range(min(2, nck)):
                stT(c_)
                if not prefix:
                    stA(c_)
            stK(0)
            for ci in range(nck):
                if ci + 2 < nck:
                    stT(ci + 2)
                    if not prefix:
                        stA(ci + 2)
                if ci + 1 < nck:
                    stK(ci + 1)
                if not prefix:
                    stO(ci)
            if not prefix:
                nh = tcols // 2
                for j in range(2):
                    act(TMP[:, j, 0:tcols], W1[:, j, 0:tcols], AF.Square, [("W1", j)], [("TMP", j)])
                    PX, pkey = next_px()
                    for h in range(2):
                        mm(PX[:, h, 0:nh], onesb[:], TMP[:, j, h * nh:(h + 1) * nh], True, True, [("TMP", j), "onesb"], [pkey],
                           signal=(h == 1))
                    act(W2[:, j, 0:tcols].rearrange("p (h n) -> p h n", h=2), PX[:, :, 0:nh], AF.Ln, [pkey, "epst"], [("W2", j)],
                        scale=1.0 / 128, bias=epst[:, 0:1])
                    act(W2[:, j, 0:tcols], W2[:, j, 0:tcols], AF.Exp, [("W2", j)], [("W2", j)], scale=-0.5)
                    tt(W1[:, j, 0:tcols], W1[:, j, 0:tcols], W2[:, j, 0:tcols], ALU.mult, [("W1", j), ("W2", j)], [("W1", j)])
                    stt(onr[:, j, 0:tcols], W1[:, j, 0:tcols], hgn[:, 0:1], gT[:, j, 0:tcols], ALU.mult, ALU.mult,
                        [("W1", j), "hgn", ("gT", j)], [("onr", j)])
            if not prefix:
                checkpoint("hg_chunks")
                out_proj(onr, "onr", 2, "w_out_a", 0, h0 * 128, first)
                checkpoint("hg_out")

        def hgrn_locals(Lx):
            d = {}
            for nm, dt in (("qT", BF16), ("W1", F32), ("W2", F32), ("KB", BF16), ("TMP", BF16), ("iT", BF16), ("gT", BF16), ("onr", BF16)):
                d[nm] = Lx.sb("h_" + nm, [128, 2, T], dt)
            d["EL"] = Lx.sb("h_EL", [128, 2, 24])
            d["Vtok"] = [Lx.sb("h_Vtok%d" % i, [64, 256], BF16) for i in range(3)]
            d["Ktok"] = [Lx.sb("h_Ktok%d" % i, [64, 256], BF16) for i in range(3)]
            d["Ktm"] = Lx.sb("h_Ktm", [64, 256], BF16)
            d["AT"] = [Lx.sb("h_AT%d" % i, [64, 2, 64], BF16) for i in range(3)]
            d["SbL"] = [Lx.sb("h_SbL%d" % i, [128, 2, 128], BF16) for i in range(3)]
            d["Ssf"] = [Lx.sb("h_Ssf%d" % i, [128, 2, 128]) for i in range(4)]
            d["Ssb"] = [Lx.sb("h_Ssb%d" % i, [128, 2, 128], BF16) for i in range(4)]
            d["Sout"] = [Lx.sb("h_Sout%d" % i, [128, 2, 128]) for i in range(2)]
            d["sloaded"] = {}
            return d

        for sbk in range(NPRE // TPRE):
            L = Local()
            Ld = dict(stage=[L.sb("stage0", [128, D]), L.sb("stage1", [128, D])])
            load_x(xpre[sbk * TPRE:(sbk + 1) * TPRE, :], TPRE, 0, Ld)
            prenorm(V_MIXPRE + 0, TPRE)
            L.free()
            L = Local()
            Lh = hgrn_locals(L)
            Lh["rmask"] = rm
            S.dma("sp", rm[:, 0:TPRE], rmpre_d[:, :], writes=["rm"])
            for g in range(8):
                hgrn_group(2 * g, TPRE, TPRE // 64, [], Lh, True, False)
            L.free()

        checkpoint("prefix")

        def attn_layer(l, p):
            nch, seqs = p["nch"], p["seqs"]
            Tp = nch * 64
            nhp = Tp // 2
            ns = len(seqs)
            L = Local()
            qh = L.sb("a_qh", [128, 4, T], BF16)
            ah = L.sb("a_ah", [128, 4, T], BF16)
            ES = L.sb("a_ES", [128, 2, 576], BF16)
            rinv = L.sb("a_rinv", [128, 576])
            Ks = [L.sb("a_Ks%d" % i, [128, 2, 512], BF16) for i in range(2)]
            Vs = [L.sb("a_Vs%d" % i, [128, 2, 512], BF16) for i in range(3)]
            KTs = [L.sb("a_KTs%d" % i, [128, 4, 256], BF16) for i in range(2)]
            ESs = [L.sb("a_ESs%d" % i, [128, 2, 8], BF16) for i in range(2)]
            rinvs = L.sb("a_rinvs", [128, 8])
            sc = 512.0 ** -0.5
            for hd in range(4):
                for b in range(2):
                    wv, wkey = wload("w_xq", l, 0, 16, hd * 512 + b * 256, 256)

                    def evq(j, P3, pkey, b=b):
                        act(qh[:, 2 * b + j, :].rearrange("p (h n) -> p h n", h=2), P3, AF.Copy, [pkey], [("qh", 2 * b + j)])
                    fm_proj(wv, wkey, 16, 2, hn, "hn", T, evq)
                for mt in range(2):
                    PX, pkey = next_px()
                    for dc in range(4):
                        for h in range(2):
                            mm(PX[:, h, 0:nhp], KT[l][:, hd * 4 + dc, mt * 128:(mt + 1) * 128], qh[:, dc, h * nhp:(h + 1) * nhp],
                               dc == 0, dc == 3, [("KT", l), ("qh", dc)], [pkey], signal=(dc == 3 and h == 1))
                    act(ES[:, mt, 0:Tp].rearrange("p (h n) -> p h n", h=2), PX[:, :, 0:nhp], AF.Exp, [pkey], [("ES", mt)], scale=sc)
                for mt in range(2):
                    for h in range(2):
                        mm(PN[:, h, 0:nhp], onesb[:], ES[:, mt, h * nhp:(h + 1) * nhp], mt == 0, mt == 1, [("ES", mt), "onesb"], ["PN"],
                           signal=(mt == 1 and h == 1))
                S.op("dve", lambda e: e.reciprocal(out=rinv[:, 0:Tp].rearrange("p (h n) -> p h n", h=2), in_=PN[:, :, 0:nhp]),
                     ["PN"], ["rinv"])
                for dc in range(4):
                    PX, pkey = next_px()
                    for mt in range(2):
                        for h in range(2):
                            mm(PX[:, h, 0:nhp], VV[l][:, mt, hd * 512 + dc * 128: hd * 512 + (dc + 1) * 128], ES[:, mt, h * nhp:(h + 1) * nhp],
                               mt == 0, mt == 1, [("VV", l), ("ES", mt)], [pkey], signal=(mt == 1 and h == 1))
                    tt(ah[:, dc, 0:Tp].rearrange("p (h n) -> p h n", h=2), PX[:, :, 0:nhp],
                       rinv[:, 0:Tp].rearrange("p (h n) -> p h n", h=2), ALU.mult, [pkey, "rinv"], [("ah", dc)])
                def stA1(si, hd=hd):
                    s = seqs[si]
                    b2 = si % 2
                    b3 = si % 3
                    S.dma("pool", Ks[b2][:], ck[l, s, :, hd * 512:(hd + 1) * 512].rearrange("(t p) d -> p t d", p=128), writes=[("Ks", b2)])
                    S.dma("pool", Vs[b3][:], cv[l, s, :, hd * 512:(hd + 1) * 512].rearrange("(t p) d -> p t d", p=128), writes=[("Vs", b3)])
                    for dc in range(4):
                        for mt in range(2):
                            tr(PT[:, dc * 256 + mt * 128: dc * 256 + (mt + 1) * 128], Ks[b2][:, mt, dc * 128:(dc + 1) * 128], identb[:, :],
                               [("Ks", b2), "identb"], ["PT"], signal=(dc == 3 and mt == 1))
                    act(KTs[b2][:].rearrange("p c m -> p (c m)"), PT[:, :], AF.Copy, ["PT"], [("KTs", b2)])

                def stA2(si, hd=hd):
                    b2 = si % 2
                    c0 = Tp + si * 8
                    for mt in range(2):
                        for dc in range(4):
                            mm(PG[:, mt * 8:(mt + 1) * 8], KTs[b2][:, dc, mt * 128:(mt + 1) * 128], qh[:, dc, c0:c0 + 8], dc == 0, dc == 3,
                               [("KTs", b2), ("qh", dc)], ["PG"], signal=(dc == 3 and mt == 1))
                    act(ESs[b2][:].rearrange("p m t -> p (m t)"), PG[:, 0:16], AF.Exp, ["PG"], [("ESs", b2)], scale=sc)

                def stB(si, hd=hd):
                    b2 = si % 2
                    b3 = si % 3
                    c0 = Tp + si * 8
                    for mt in range(2):
                        mm(PN[:, 0, 0:8], onesb[:], ESs[b2][:, mt, :], mt == 0, mt == 1, [("ESs", b2), "onesb"], ["PN"], signal=(mt == 1))
                    S.op("dve", lambda e: e.reciprocal(out=rinvs[:], in_=PN[:, 0, 0:8]), ["PN"], ["rinvs"])
                    for dc in range(4):
                        for mt in range(2):
                            mm(PN[:, 1, dc * 8:(dc + 1) * 8], Vs[b3][:, mt, dc * 128:(dc + 1) * 128], ESs[b2][:, mt, :], mt == 0, mt == 1,
                               [("Vs", b3), ("ESs", b2)], ["PN"], signal=(dc == 3 and mt == 1))
                    tt(ah[:, :, c0:c0 + 8], PN[:, 1, 0:32].rearrange("p (c t) -> p c t", t=8),
                       rinvs[:].unsqueeze(1).broadcast_to([128, 4, 8]), ALU.mult, ["PN", "rinvs"], [("ah", dc_) for dc_ in range(4)])

                stA1(0)
                stA2(0)
                for si in range(ns):
                    if si + 1 < ns:
                        stA1(si + 1)
                    stB(si)
                    if si + 1 < ns:
                        stA2(si + 1)
                out_proj(ah, "ah", 4, "w_xo", l, hd * 512, hd == 0)
            L.free()

        def mlp_layer(l):
            L = Local()
            hT = L.sb("m_hT", [128, 8, T], BF16)
            rr = [L.sb("m_r%d" % i, [128, T], BF16) for i in range(2)]
            for g in range(8):
                for b in range(4):
                    wv, wkey = wload("w_up", l, 0, 16, g * 1024 + b * 256, 256)

                    def evu(j, P3, pkey, b=b):
                        i2 = st.get("rr", 0) % 2
                        st["rr"] = st.get("rr", 0) + 1
                        r3 = rr[i2][:, :].rearrange("p (h n) -> p h n", h=2)
                        act(r3, P3, AF.Relu, [pkey], [("rr", i2)])
                        tt(hT[:, 2 * b + j, :], rr[i2][:, :], rr[i2][:, :], ALU.mult, [("rr", i2)], [("hT", 2 * b + j)])
                    fm_proj(wv, wkey, 16, 2, hn, "hn", T, evu)
                out_proj(hT, "hT", 8, "w_down", l, g * 1024, g == 0)
            L.free()

        def pool_layer(p, pi):
            nch, seqs = p["nch"], p["seqs"]
            Tp = nch * 64
            ns = len(seqs)
            Ts = ns * 8
            LE = 15 + Tp + 23 * ns
            L = Local()
            Ec = [L.sb("p_Ec%d" % i, [128, 803]) for i in range(2)]
            Wa = L.sb("p_Wa", [128, 803])
            Wb = L.sb("p_Wb", [128, 803])
            pl = L.sb("p_pl", [128, 4, T], BF16)
            mx = L.sb("p_mx", [128, 4, T], BF16)
            invc = L.sb("p_invc", [128, T])
            ust = L.sb("p_ust", [128, 4, 96])
            oldst = L.sb("p_oldst", [128, 512])
            Ld = dict(stage=[L.sb("p_stg0", [128, 512]), L.sb("p_stg1", [128, 512])])
            for gi in range(4):
                w = 2 << gi
                S.dma("sp", invc[:], invc_d[pi, gi, :, :], writes=["invc"])
                for b in range(2):
                    wv, wkey = wload("w_in_b", 0, 0, 16, gi * 512 + b * 256, 256)

                    def evp(j, P3, pkey, b=b, gi=gi, w=w):
                        cg = 2 * b + j
                        c = gi * 4 + cg
                        i2 = st.get("ec", 0) % 2
                        st["ec"] = st.get("ec", 0) + 1
                        E = Ec[i2]
                        ke = ("Ec", i2)
                        act(Wb[:, 0:T].rearrange("p (h n) -> p h n", h=2), P3, AF.Copy, [pkey], ["Wb"])
                        S.op("dve", lambda e: e.tensor_copy(out=E[:, 15:15 + Tp], in_=Wb[:, 0:Tp]), ["Wb"], [ke])
                        S.op("dve", lambda e: e.tensor_copy(out=E[:, 15 + Tp:LE].rearrange("p (s r) -> p s r", r=23)[:, :, 15:23],
                                                            in_=Wb[:, Tp:T].rearrange("p (s r) -> p s r", r=8)), ["Wb"], [ke])
                        S.op("dve", lambda e: e.tensor_copy(out=ust[:, cg, 0:Ts], in_=Wb[:, Tp:T]), ["Wb"], [("ust", cg)])
                        act(E[:, 0:15], tailbuf[:, c, :], AF.Copy, [("tail", c)], [ke])
                        if pi == 0:
                            S.op("dve", lambda e: e.tensor_scalar(out=E[:, 15:79], in0=E[:, 15:79], scalar1=flag[:, 0:1],
                                                                  scalar2=None, op0=ALU.mult), [ke, "flag"], [ke])
                        act(tailbuf[:, c, :], E[:, Tp:Tp + 15], AF.Copy, [ke], [("tail", c)])
                        for s0 in range(0, ns, 8):
                            n8 = min(8, ns - s0)
                            if cg == 0:
                                pass
                            PXo, pko = next_px()
                            S.dma("sp", oldst[0:n8 * 15, 0:128],
                                  spool[seqs[s0]:seqs[s0] + n8, :, c * 128:(c + 1) * 128].rearrange("s r d -> (s r) d"),
                                  writes=["oldst"])
                            tr(PXo[:, 0, 0:n8 * 15], oldst[0:n8 * 15, 0:128], identf[0:n8 * 15, 0:n8 * 15], ["oldst", "identf"], [pko])
                            act(E[:, 15 + Tp + 23 * s0:15 + Tp + 23 * (s0 + n8)].rearrange("p (s r) -> p s r", r=23)[:, :, 0:15],
                                PXo[:, 0, 0:n8 * 15].rearrange("p (s r) -> p s r", r=15), AF.Copy, [pko], [ke])
                        cur = E
                        ck_ = ke
                        bufs = [(Wa, "Wa"), (Wb, "Wb")]
                        lo = 0
                        for k in range(gi + 1):
                            sh_ = 1 << k
                            nb, nk_ = bufs[k % 2]
                            lo2 = lo + sh_
                            tt(nb[:, lo2:LE], cur[:, lo2:LE], cur[:, lo2 - sh_:LE - sh_], ALU.add, [ck_], [nk_])
                            cur, ck_, lo = nb, nk_, lo2
                        oth, ko = bufs[(gi + 1) % 2]
                        tt(oth[:, 0:Tp], cur[:, 15:15 + Tp], invc[:, 0:Tp], ALU.mult, [ck_, "invc"], [ko])
                        tt(pl[:, cg, 0:Tp], oth[:, 0:Tp], E[:, 15:15 + Tp], ALU.subtract, [ko, ke], [("pl", cg)])
                        cs3 = cur[:, 15 + Tp:LE].rearrange("p (s r) -> p s r", r=23)[:, :, 15:23]
                        es3 = E[:, 15 + Tp:LE].rearrange("p (s r) -> p s r", r=23)[:, :, 15:23]
                        o3 = oth[:, Tp:T].rearrange("p (s r) -> p s r", r=8)
                        tt(o3, cs3, invc[:, Tp:T].rearrange("p (s r) -> p s r", r=8), ALU.mult, [ck_, "invc"], [ko])
                        tt(pl[:, cg, Tp:T].rearrange("p (s r) -> p s r", r=8), o3, es3, ALU.subtract, [ko, ke], [("pl", cg)])
                    fm_proj(wv, wkey, 16, 2, hn, "hn", T, evp)
                for s0 in range(0, ns, 12):
                    n12 = min(12, ns - s0)
                    PXo, pko = next_px()
                    for cg in range(4):
                        tr(PXo[0:n12 * 8, cg // 2, (cg % 2) * 128:(cg % 2) * 128 + 128], ust[:, cg, s0 * 8:(s0 + n12) * 8], identf[:, :],
                           [("ust", cg), "identf"], [pko], signal=(cg == 3))
                    stg = Ld["stage"][gi % 2]
                    sk = ("pstg", gi % 2)
                    act(stg[0:n12 * 8, :].rearrange("p (h n) -> p h n", h=2), PXo[0:n12 * 8, :, 0:256], AF.Copy, [pko], [sk])
                    for si in range(n12):
                        s = seqs[s0 + si]
                        S.dma("sp", pool_s[s, 7:15, gi * 512:(gi + 1) * 512], stg[si * 8:(si + 1) * 8, :], reads=[sk])
                for b in range(2):
                    wv, wkey = wload("pool_w", gi, 0, 4, b * 256, 256)

                    def evm(j, P3, pkey, b=b, gi=gi):
                        cg = 2 * b + j
                        act(mx[:, cg, :].rearrange("p (h n) -> p h n", h=2), P3, AF.Copy, [pkey], [("mx", cg)],
                            scale=vec[:, V_PSCALE, gi * 4 + cg:gi * 4 + cg + 1])
                    fm_proj(wv, wkey, 4, 2, pl, "pl", T, evm)
                out_proj(mx, "mx", 4, "w_out_b", 0, gi * 512, gi == 0)
            L.free()

        S.dma("sp", pool_s[:, 0:7, :], spool[:, 8:15, :])

        for pi, p in enumerate(PASSES):
            nch, seqs = p["nch"], p["seqs"]
            Tp = nch * 64
            ns = len(seqs)
            L = Local()
            Ld = dict(stage=[L.sb("stage0", [128, D]), L.sb("stage1", [128, D])])
            load_x(xp[p["row0"]:p["row0"] + Tp, :], Tp, 0, Ld)
            load_x(xs[seqs[0] * 8:(seqs[0] + ns) * 8, :], ns * 8, Tp, Ld)
            S.dma("sp", rm[:, :], rm_d[pi, :, :], writes=["rm"])
            L.free()
            checkpoint("p%dload" % pi)
            for l in range(2):
                prenorm(V_MIXPRE + l, T)
                if l == 0:
                    L = Local()
                    Lh = hgrn_locals(L)
                    Lh["rmask"] = rm
                    for g in range(8):
                        hgrn_group(2 * g, T, nch, seqs, Lh, False, g == 0)
                    L.free()
                else:
                    pool_layer(p, pi)
                if pi == 0 and l == 0:
                    dbg_dump("d_hg", Fb[:, :, Tp:Tp + 32], [128, 16, 32], [("F", c_) for c_ in range(16)])
                checkpoint("p%dl%dmix0" % (pi, l))
                postnorm(V_MIXPOST + l)
                checkpoint("p%dl%dmix" % (pi, l))
                prenorm(V_XPRE + l, T)
                attn_layer(l, p)
                if pi == 0 and l == 0:
                    dbg_dump("d_at", Fb[:, :, Tp:Tp + 32], [128, 16, 32], [("F", c_) for c_ in range(16)])
                postnorm(V_XPOST + l)
                checkpoint("p%dl%dattn" % (pi, l))
                prenorm(V_MLPPRE + l, T)
                mlp_layer(l)
                postnorm(V_MLPPOST + l)
                checkpoint("p%dl%dmlp" % (pi, l))
            L = Local()
            Ld = dict(stage=[L.sb("stage0", [128, D]), L.sb("stage1", [128, D])])
            if pi == 0:
                store_rows(xT, "x", 64, Tp - 64, y_p[0:Tp - 64, :], Ld)
            else:
                store_rows(xT, "x", 0, Tp, y_p[512:1024, :], Ld)
            store_rows(xT, "x", Tp, ns * 8, y_s[seqs[0] * 8:(seqs[0] + ns) * 8, :], Ld)
            L.free()

        for h in range(16):
            S.dma("sp", S_p[h, :, :], Sst[:, h, :], reads=[("Sst", h)])
        L = Local()
        Ld = dict(stage=[L.sb("stage0", [128, D]), L.sb("stage1", [128, D])])
        store_rows(tailbuf, "tail", 0, 15, pool_p[:, :], Ld)
        L.free()
    try:
        body()
    except _Stop:
        for Lx in list(reversed(live)):
            Lx.free()
    S.barrier()
    S.close()
    for cm in reversed(cms):
        cm.__exit__(None, None, None)
    S.specs = specs
    return nc, S


_CACHE = {}


def kernel(x_prompt, x_sample, state_hgrn, state_pool, cache_mem_k, cache_mem_v, mem_prompt,
           w_in_a, hg_lb_logits, hg_norm, w_out_a, w_in_b, pool_w, pool_scale, w_out_b,
           norm_mem, w_xq, w_xkv, w_xo, norm_mix_pre, norm_mix_post, norm_x_pre, norm_x_post,
           norm_mlp_pre, norm_mlp_post, w_up, w_down):
    f = np.float32
    A = lambda a: np.ascontiguousarray(np.asarray(a, dtype=f))
    x_prompt, x_sample = A(x_prompt), A(x_sample)
    if "nc" not in _CACHE:
        _CACHE["nc"], S_ = build_program()
        _CACHE["specs"] = S_.specs
    nc = _CACHE["nc"]
    vecs = [norm_mix_pre, norm_mix_post, norm_x_pre, norm_x_post, norm_mlp_pre, norm_mlp_post, norm_mem]
    rows = []
    for v in vecs:
        v = A(v)
        rows += [v[0], v[1]]
    rows.append(A(pool_scale)[0])
    vec = np.ascontiguousarray(np.stack(rows, 0).reshape(15, 16, 128).transpose(2, 0, 1))
    lbl = np.ascontiguousarray(A(hg_lb_logits).reshape(3, 16, 128).transpose(2, 0, 1))
    hgn = np.ascontiguousarray(A(hg_norm)[0].reshape(128, 1))
    ident = np.eye(128, dtype=f)
    mc = np.triu(np.ones((64, 64), f))
    sid = np.arange(32) // 8
    ms = (np.triu(np.ones((32, 32), f)) * (sid[:, None] == sid[None, :])).astype(f)
    seqm = (sid[:, None] == np.arange(4)[None, :]).astype(f)
    rm = np.ones((2, 128, T), f)
    invc = np.ones((2, 4, 128, T), f)
    rmpre = np.ones((128, TPRE), f)
    rmpre[:, 0::64] = 0.0
    wsrc = dict(w_in_a=A(w_in_a), w_out_a=A(w_out_a), w_in_b=A(w_in_b), pool_w=A(pool_w)[0],
                w_out_b=A(w_out_b), w_xq=A(w_xq), w_xkv=A(w_xkv), w_xo=A(w_xo), w_up=A(w_up), w_down=A(w_down))
    specs = _CACHE["specs"]
    assert len(specs) <= NBLK, len(specs)
    wblk = np.zeros((NBLK, 128, 4096), f)
    for (name, idx, r0, nk, c0, C), bid in specs.items():
        blk = wsrc[name][idx][r0:r0 + nk * 128, c0:c0 + C]
        wblk[bid, :, 0:nk * C] = blk.reshape(nk, 128, C).transpose(1, 0, 2).reshape(128, nk * C)
    weights = dict(wblk=wblk)
    state_hgrn, state_pool = A(state_hgrn), A(state_pool)
    cache_mem_k, cache_mem_v, mem_prompt = A(cache_mem_k), A(cache_mem_v), A(mem_prompt)
    in_maps = []
    for c in range(8):
        b, half = c // 2, c % 2
        rmc = rm.copy()
        ivc = invc.copy()
        for pi, p in enumerate(PASSES):
            Tp = p["nch"] * 64
            rmc[pi, :, 0:Tp:64] = 0.0
            rmc[pi, :, Tp::8] = 0.0
            pos = half * 1024 - 64 + p["row0"] + np.arange(Tp)
            for gi in range(4):
                w = 2 << gi
                ivc[pi, gi, :, 0:Tp] = (1.0 / np.minimum(w, np.maximum(pos, 0) + 1))[None, :]
                ivc[pi, gi, :, Tp:] = 1.0 / w
        xpc = np.zeros((1088, 2048), f)
        lo = half * 1024 - 64
        if lo < 0:
            xpc[64:] = x_prompt[b, 0:1024]
        else:
            xpc[:] = x_prompt[b, lo:lo + 1088]
        xpre = np.zeros((NPRE, 2048), f)
        if half == 1:
            xpre[:] = x_prompt[b, 0:NPRE]
        s0 = c * 16
        m = dict(
            xp=xpc, xpre=xpre, xs=np.ascontiguousarray(x_sample[s0:s0 + 16].reshape(128, 2048)),
            sh=np.ascontiguousarray(state_hgrn[0, s0:s0 + 16]), spool=np.ascontiguousarray(state_pool[0, s0:s0 + 16]),
            ck=np.ascontiguousarray(cache_mem_k[:, s0:s0 + 16].reshape(2, 16, 256, 2048)),
            cv=np.ascontiguousarray(cache_mem_v[:, s0:s0 + 16].reshape(2, 16, 256, 2048)),
            mem=np.ascontiguousarray(mem_prompt[b]), vec=vec, lbl=lbl, hgn=hgn,
            flag=np.full((128, 1), float(half), f), ident=ident, mc=mc, ms=ms, seqm=seqm, rm=rmc, rmpre=rmpre, invc=ivc,
        )
        m.update(weights)
        in_maps.append(m)
    if _CACHE.get("dbg_only_maps"):
        return in_maps
    res = run_bass_kernel_spmd(nc, in_maps, core_ids=list(range(8)))
    R = res.results
    y_prompt = np.stack([np.concatenate([R[2 * b]["y_p"], R[2 * b + 1]["y_p"]], 0) for b in range(4)], 0)
    y_sample = np.concatenate([R[c]["y_s"] for c in range(8)], 0).reshape(128, 8, 2048)
    st_h_p = np.stack([R[2 * b + 1]["S_p"] for b in range(4)], 0)[None]
    st_p_p = np.stack([R[2 * b + 1]["pool_p"] for b in range(4)], 0)[None]
    mk = np.stack([R[2 * b]["mk_o"] for b in range(4)], 1).reshape(2, 4, 256, 4, 512)
    mv = np.stack([R[2 * b]["mv_o"] for b in range(4)], 1).reshape(2, 4, 256, 4, 512)
    st_h_s = np.concatenate([R[c]["S_s"] for c in range(8)], 0)[None]
    st_p_s = np.concatenate([R[c]["pool_s"] for c in range(8)], 0)[None]
    return (y_prompt.astype(f), y_sample.astype(f), st_h_p.astype(f), st_p_p.astype(f), mk.astype(f), mv.astype(f),
            st_h_s.astype(f), st_p_s.astype(f))
```

```python
import os
import numpy as np
import concourse.bass as bass
import concourse.mybir as mybir
from concourse.bass_utils import run_bass_kernel_spmd

F32 = mybir.dt.float32
BF16 = mybir.dt.bfloat16
AF = mybir.ActivationFunctionType
ALU = mybir.AluOpType

D = 2048
NCH = 16
T = 608
EPS = 1e-6
PASSES = [dict(nch=9, seqs=list(range(0, 4)), row0=0), dict(nch=8, seqs=list(range(4, 16)), row0=576)]
NPRE = 960
NBLK = 256
TPRE = 320
V_MIXPRE, V_MIXPOST, V_XPRE, V_XPOST, V_MLPPRE, V_MLPPOST, V_MEM, V_PSCALE = 0, 2, 4, 6, 8, 10, 12, 14


class Sync:
    def __init__(self, nc, ndma_sems=12):
        self.nc = nc
        self.eng = {"pe": nc.tensor, "act": nc.scalar, "dve": nc.vector, "pool": nc.gpsimd, "sp": nc.sync}
        self._cms = []
        self.sem = {}
        for e in self.eng:
            cm = nc.semaphore("sem_" + e)
            self.sem[e] = cm.__enter__()
            self._cms.append(cm)
        self.cnt = {e: 0 for e in self.eng}
        self.known = {e: {} for e in self.eng}
        self.dsem = {}
        self.dval = {}
        self.drot = {}
        for q in ("sp", "pool", "act"):
            lst = []
            for i in range(ndma_sems):
                cm = nc.semaphore("dsem_%s_%d" % (q, i))
                lst.append(cm.__enter__())
                self._cms.append(cm)
            self.dsem[q] = lst
            self.dval[q] = [0] * ndma_sems
            self.drot[q] = 0
        self.last_w = {}
        self.readers = {}
        self.ninstr = 0

    def close(self):
        for cm in reversed(self._cms):
            cm.__exit__(None, None, None)

    def _wait(self, e, tok):
        if tok[0] == "e":
            _, pe, n = tok
            if pe == e and e == "pe":
                return
            key = pe
            semh = self.sem[pe]
            val = n
        else:
            _, q, idx, val = tok
            key = (q, idx)
            semh = self.dsem[q][idx]
        if self.known[e].get(key, 0) >= val:
            return
        self.eng[e].wait_ge(semh, val)
        self.known[e][key] = val

    def _deps(self, e, reads, writes):
        toks = []
        for r in reads:
            t = self.last_w.get(r)
            if t is not None:
                toks.append(t)
        for w in writes:
            t = self.last_w.get(w)
            if t is not None:
                toks.append(t)
            toks.extend(self.readers.get(w, ()))
        for t in toks:
            self._wait(e, t)

    def _record(self, tok, reads, writes):
        for r in reads:
            lst = self.readers.setdefault(r, [])
            if tok not in lst:
                lst.append(tok)
        for w in writes:
            self.last_w[w] = tok
            self.readers[w] = []

    def op(self, e, fn, reads=(), writes=(), signal=True):
        self._deps(e, reads, writes)
        ins = fn(self.eng[e])
        self.ninstr += 1
        if signal:
            self.cnt[e] += 1
            ins.then_inc(self.sem[e], 1)
            tok = ("e", e, self.cnt[e])
        else:
            tok = ("e", e, self.cnt[e] + 1)
        self._record(tok, reads, writes)
        return ins

    def dma(self, q, out, in_, reads=(), writes=(), **kw):
        self._deps(q, reads, writes)
        idx = self.drot[q]
        self.drot[q] = (idx + 1) % len(self.dsem[q])
        if self.dval[q][idx] > 0:
            self._wait(q, ("d", q, idx, self.dval[q][idx]))
        ins = self.eng[q].dma_start(out=out, in_=in_, **kw)
        self.ninstr += 1
        self.dval[q][idx] += 16
        ins.then_inc(self.dsem[q][idx], 16)
        tok = ("d", q, idx, self.dval[q][idx])
        self._record(tok, reads, writes)
        return tok

    def barrier(self, final=False):
        for e in self.eng:
            if e == "pe" and not final:
                continue
            for q in self.dsem:
                for idx, v in enumerate(self.dval[q]):
                    if v > 0:
                        self._wait(e, ("d", q, idx, v))
            for pe in self.eng:
                if pe != e and self.cnt[pe] > 0:
                    self._wait(e, ("e", pe, self.cnt[pe]))
        for k in list(self.readers):
            if not (isinstance(k, str) and k.startswith("P")):
                self.readers[k] = []


class _Stop(Exception):
    pass


def build_program(stop=None):
    nc = bass.Bass("TRN2", target_bir_lowering=False)
    live = []

    def checkpoint(name):
        if stop == name:
            raise _Stop()

    def din(name, shape):
        return nc.dram_tensor(name, list(shape), F32, kind="ExternalInput").ap()

    def dout(name, shape):
        return nc.dram_tensor(name, list(shape), F32, kind="ExternalOutput").ap()

    xp = din("xp", [1088, D])
    xpre = din("xpre", [NPRE, D])
    xs = din("xs", [128, D])
    sh = din("sh", [16, 16, 128, 128])
    spool = din("spool", [16, 15, D])
    ck = din("ck", [2, 16, 256, D])
    cv = din("cv", [2, 16, 256, D])
    mem = din("mem", [256, D])
    wblk = din("wblk", [NBLK, 128, 4096])
    vec_d = din("vec", [128, 15, 16])
    lbl_d = din("lbl", [128, 3, 16])
    hgn_d = din("hgn", [128, 1])
    flag_d = din("flag", [128, 1])
    ident_d = din("ident", [128, 128])
    mc_d = din("mc", [64, 64])
    ms_d = din("ms", [32, 32])
    seqm_d = din("seqm", [32, 4])
    rm_d = din("rm", [2, 128, T])
    rmpre_d = din("rmpre", [128, TPRE])
    invc_d = din("invc", [2, 4, 128, T])

    y_p = dout("y_p", [1024, D])
    y_s = dout("y_s", [128, D])
    S_p = dout("S_p", [16, 128, 128])
    pool_p = dout("pool_p", [15, D])
    mk_o = dout("mk_o", [2, 256, D])
    mv_o = dout("mv_o", [2, 256, D])
    S_s = dout("S_s", [16, 16, 128, 128])
    pool_s = dout("pool_s", [16, 15, D])

    S = Sync(nc)
    cms = []

    def sb(name, shape, dt=F32):
        cm = nc.sbuf_tensor("s_" + name, list(shape), dt)
        cms.append(cm)
        return cm.__enter__()

    def psum(name, shape, dt=F32):
        cm = nc.psum_tensor("p_" + name, list(shape), dt)
        cms.append(cm)
        return cm.__enter__()

    class Local:
        def __init__(self):
            self.l = []
            live.append(self)

        def sb(self, name, shape, dt=F32):
            st["uid"] = st.get("uid", 0) + 1
            cm = nc.sbuf_tensor("l%d_%s" % (st["uid"], name), list(shape), dt)
            self.l.append(cm)
            return cm.__enter__()

        def free(self):
            S.barrier()
            for cm in reversed(self.l):
                cm.__exit__(None, None, None)
            self.l = []
            live.remove(self)

    xT = sb("xT", [128, NCH, T])
    hn = sb("hn", [128, NCH, T], BF16)
    Fb = sb("Fb", [128, NCH, T])
    ring = [sb("ring%d" % i, [128, 4096], BF16) for i in range(3)]
    KT = [sb("KT%d" % l, [128, 16, 256], BF16) for l in range(2)]
    VV = [sb("VV%d" % l, [128, 2, D], BF16) for l in range(2)]
    Sst = sb("Sst", [128, 16, 128])
    vec = sb("vec", [128, 15, 16])
    lbl = sb("lbl", [128, 3, 16])
    lbv = sb("lbv", [128, 16])
    oml = sb("oml", [128, 16])
    hgn = sb("hgn", [128, 1])
    flag = sb("flag", [128, 1])
    identf = sb("identf", [128, 128])
    identb = sb("identb", [128, 128], BF16)
    onesb = sb("onesb", [128, 128], BF16)
    epst = sb("epst", [128, 1])
    mc = sb("mc", [64, 64])
    ms = sb("ms", [32, 32])
    seqm = sb("seqm", [32, 4])
    rm = sb("rm", [128, T])
    rstd = sb("rstd", [128, T])
    sq = [sb("sq%d" % i, [128, T], BF16) for i in range(2)]
    tailbuf = sb("tailbuf", [128, NCH, 15])

    PA = psum("PA", [128, 2, 512])
    PB = psum("PB", [128, 2, 512])
    PN = psum("PN", [128, 2, 512])
    PG = psum("PG", [128, 512])
    PT = psum("PT", [128, 1024], BF16)
    st = dict(ring=0, px=0, sq=0)

    def act(out, in_, func, reads, writes, **kw):
        S.op("act", lambda e: e.activation(out=out, in_=in_, func=func, **kw), reads, writes)

    def tt(out, in0, in1, op, reads, writes):
        S.op("dve", lambda e: e.tensor_tensor(out=out, in0=in0, in1=in1, op=op), reads, writes)

    def tsc(out, in0, s1, s2, op0, op1, reads, writes):
        S.op("dve", lambda e: e.tensor_scalar(out=out, in0=in0, scalar1=s1, scalar2=s2, op0=op0, op1=op1), reads, writes)

    def stt(out, in0, scalar, in1, op0, op1, reads, writes):
        S.op("dve", lambda e: e.scalar_tensor_tensor(out=out, in0=in0, scalar=scalar, in1=in1, op0=op0, op1=op1),
             reads, writes)

    def mm(out, lhsT, rhs, start, stop, reads, writes, signal):
        S.op("pe", lambda e: e.matmul(out, lhsT=lhsT, rhs=rhs, start=start, stop=stop, skip_group_check=True),
             reads, writes, signal=signal)

    def tr(out, in_, ident, reads, writes, signal=True):
        S.op("pe", lambda e: e.transpose(out=out, in_=in_, identity=ident), reads, writes, signal=signal)

    def halves(ap, tt_):
        return ap.rearrange("p (h n) -> p h n", h=2)

    specs = {}

    def wload(name, idx, r0, nk, c0, C):
        spec = (name, idx, r0, nk, c0, C)
        if spec not in specs:
            specs[spec] = len(specs)
        bid = specs[spec]
        slot = st["ring"] % 3
        st["ring"] += 1
        view = ring[slot][:, 0:nk * C].rearrange("p (k c) -> p k c", c=C)
        S.dma("pool", ring[slot][:, 0:nk * C], wblk[bid, :, 0:nk * C], writes=[("ring", slot)])
        return view, ("ring", slot)

    def next_px():
        st["px"] += 1
        return (PA, "PA") if st["px"] % 2 else (PB, "PB")

    def fm_proj(wv, wkey, nk, nj, src, skey, tcols, evac):
        nh = tcols // 2
        for j in range(nj):
            PX, pkey = next_px()
            for kc in range(nk):
                for h in range(2):
                    mm(PX[:, h, 0:nh], wv[:, kc, j * 128:(j + 1) * 128], src[:, kc, h * nh:(h + 1) * nh],
                       kc == 0, kc == nk - 1, [wkey, (skey, kc)], [pkey], signal=(kc == nk - 1 and h == 1))
            evac(j, PX[:, :, 0:nh], pkey)

    def norm_stats(src, skey, tcols, scale):
        nh = tcols // 2
        for c in range(NCH):
            sqb = sq[st["sq"] % 2]
            sk = ("sq", st["sq"] % 2)
            st["sq"] += 1
            act(sqb[:, 0:tcols], src[:, c, 0:tcols], AF.Square, [(skey, c)], [sk])
            for h in range(2):
                mm(PN[:, h, 0:nh], onesb[:], sqb[:, h * nh:(h + 1) * nh], c == 0, c == NCH - 1, [sk, "onesb"], ["PN"],
                   signal=(h == 1))
        r3 = rstd[:, 0:tcols].rearrange("p (h n) -> p h n", h=2)
        act(r3, PN[:, :, 0:nh], AF.Ln, ["PN", "epst"], ["rstd"], scale=scale, bias=epst[:, 0:1])
        act(rstd[:, 0:tcols], rstd[:, 0:tcols], AF.Exp, ["rstd"], ["rstd"], scale=-0.5)

    def prenorm(vidx, tcols):
        norm_stats(xT, "x", tcols, 1.0 / D)
        for c in range(NCH):
            stt(hn[:, c, 0:tcols], xT[:, c, 0:tcols], vec[:, vidx, c:c + 1], rstd[:, 0:tcols], ALU.mult, ALU.mult,
                [("x", c), "rstd", "vec"], [("hn", c)])

    def postnorm(vidx):
        norm_stats(Fb, "F", T, 1.0 / D)
        for c in range(NCH):
            stt(Fb[:, c, :], Fb[:, c, :], vec[:, vidx, c:c + 1], rstd[:, :], ALU.mult, ALU.mult,
                [("F", c), "rstd", "vec"], [("F", c)])
            tt(xT[:, c, :], xT[:, c, :], Fb[:, c, :], ALU.add, [("x", c), ("F", c)], [("x", c)])

    def out_proj(src, skey, nk, wname, widx, wr0, first):
        C = 4096 // nk
        nj = C // 128
        for b in range(D // C):
            wv, wkey = wload(wname, widx, wr0, nk, b * C, C)

            def evac(j, P3, pkey, b=b):
                oc = b * nj + j
                dst = Fb[:, oc, :].rearrange("p (h n) -> p h n", h=2)
                if first:
                    act(dst, P3, AF.Copy, [pkey], [("F", oc)])
                else:
                    tt(dst, dst, P3, ALU.add, [pkey, ("F", oc)], [("F", oc)])
            fm_proj(wv, wkey, nk, nj, src, skey, T, evac)

    def load_x(rows_ap, nrows, col0, L):
        r = 0
        i = 0
        while r < nrows:
            n = min(128, nrows - r)
            stg = L["stage"][i % 2]
            sk = ("stage", i % 2)
            S.dma("sp", stg[0:n, :], rows_ap[r:r + n, :], writes=[sk])
            for g in range(4):
                PX, pkey = next_px()
                for q in range(4):
                    c = g * 4 + q
                    tr(PX[:, q // 2, (q % 2) * 256:(q % 2) * 256 + n], stg[0:n, c * 128:(c + 1) * 128], identf[0:n, 0:n],
                       [sk, "identf"], [pkey], signal=(q == 3))
                src = PX[:].rearrange("p h (q n) -> p (h q) n", q=2)[:, :, 0:n]
                act(xT[:, g * 4:(g + 1) * 4, col0 + r:col0 + r + n], src, AF.Copy, [pkey],
                    [("x", c_) for c_ in range(g * 4, g * 4 + 4)])
            r += n
            i += 1

    def store_rows(srcT, skey, col0, nrows, dst_ap, L, nchunks=NCH, dcol0=0):
        r = 0
        i = st.get("stg", 0)
        while r < nrows:
            n = min(128, nrows - r)
            stg = L["stage"][i % 2]
            sk = ("stage", i % 2)
            for g in range(nchunks // 4):
                PX, pkey = next_px()
                for q in range(4):
                    c = g * 4 + q
                    tr(PX[0:n, q // 2, (q % 2) * 128:(q % 2) * 128 + 128], srcT[:, c, col0 + r:col0 + r + n], identf[:, :],
                       [(skey, c), "identf"], [pkey], signal=(q == 3))
                src = PX[0:n, :, 0:256]
                dst = stg[0:n, g * 512:(g + 1) * 512].rearrange("p (h n) -> p h n", h=2)
                act(dst, src, AF.Copy, [pkey], [sk])
            S.dma("sp", dst_ap[r:r + n, dcol0:dcol0 + nchunks * 128], stg[0:n, 0:nchunks * 128], reads=[sk])
            r += n
            i += 1
        st["stg"] = i

    def dbg_dump(name, ap, shape, reads):
        if os.environ.get("DBGDUMP"):
            d = nc.dram_tensor(name, list(shape), F32, kind="ExternalOutput").ap()
            S.dma("sp", d, ap, reads=reads)

    S.dma("sp", vec[:], vec_d[:, :, :], writes=["vec"])
    S.dma("sp", lbl[:], lbl_d[:, :, :], writes=["lbl"])
    S.dma("sp", hgn[:], hgn_d[:, :], writes=["hgn"])
    S.dma("sp", flag[:], flag_d[:, :], writes=["flag"])
    S.dma("sp", identf[:], ident_d[:, :], writes=["identf"])
    S.dma("pool", identb[:], ident_d[:, :], writes=["identb"])
    S.dma("sp", mc[:], mc_d[:, :], writes=["mc"])
    S.dma("sp", ms[:], ms_d[:, :], writes=["ms"])
    S.dma("sp", seqm[:], seqm_d[:, :], writes=["seqm"])
    S.op("dve", lambda e: e.memset(onesb[:], 1.0), [], ["onesb"])
    S.op("dve", lambda e: e.memset(epst[:], EPS), [], ["epst"])
    S.op("dve", lambda e: e.memset(Sst[:], 0.0), [], ["Sst"])
    S.op("dve", lambda e: e.memset(tailbuf[:], 0.0), [], ["tail"])
    act(lbl[:], lbl[:], AF.Exp, ["lbl"], ["lbl"])
    tt(oml[:], lbl[:, 0, :], lbl[:, 1, :], ALU.add, ["lbl"], ["oml"])
    tt(oml[:], oml[:], lbl[:, 2, :], ALU.add, ["oml", "lbl"], ["oml"])
    S.op("dve", lambda e: e.reciprocal(out=oml[:], in_=oml[:]), ["oml"], ["oml"])
    tt(lbv[:], lbl[:, 0, :], oml[:], ALU.mult, ["lbl", "oml"], ["lbv"])
    tsc(oml[:], lbv[:], -1.0, 1.0, ALU.mult, ALU.add, ["lbv"], ["oml"])
    S.barrier()

    def body():
        L = Local()
        Ld = dict(stage=[L.sb("stage0", [128, D]), L.sb("stage1", [128, D])])
        memn = L.sb("memn", [128, NCH, 256], BF16)
        checkpoint("const")
        load_x(mem, 256, 0, Ld)
        checkpoint("memload")
        norm_stats(xT, "x", 256, 1.0 / D)
        checkpoint("memnorm")
        for l in range(2):
            for c in range(NCH):
                stt(memn[:, c, :], xT[:, c, 0:256], vec[:, V_MEM + l, c:c + 1], rstd[:, 0:256], ALU.mult, ALU.mult,
                    [("x", c), "rstd", "vec"], [("memn", c)])
            checkpoint("memn%d" % l)
            for kv in range(2):
                outd = mk_o if kv == 0 else mv_o
                for b in range(8):
                    wv, wkey = wload("w_xkv", l, 0, 16, kv * D + b * 256, 256)
                    for mt in range(2):
                        PX, pkey = next_px()
                        for kc in range(NCH):
                            mm(PX[:, 0, 0:256], memn[:, kc, mt * 128:(mt + 1) * 128], wv[:, kc, :], kc == 0, kc == NCH - 1,
                               [wkey, ("memn", kc)], [pkey], signal=(kc == NCH - 1))
                        stg = Ld["stage"][mt]
                        act(stg[:, b * 256:(b + 1) * 256], PX[:, 0, 0:256], AF.Copy, [pkey], [("stage", mt)])
                        if kv == 1:
                            S.op("dve", lambda e, mt=mt, b=b, stg=stg: e.tensor_copy(out=VV[l][:, mt, b * 256:(b + 1) * 256],
                                                                                     in_=stg[:, b * 256:(b + 1) * 256]),
                                 [("stage", mt)], [("VV", l)])
                    checkpoint("tok%d%d%d" % (l, kv, b))
                    if kv == 0:
                        for j in range(2):
                            PX, pkey = next_px()
                            for kc in range(NCH):
                                mm(PX[:, 0, 0:256], wv[:, kc, j * 128:(j + 1) * 128], memn[:, kc, :], kc == 0, kc == NCH - 1,
                                   [wkey, ("memn", kc)], [pkey], signal=(kc == NCH - 1))
                            act(KT[l][:, b * 2 + j, :], PX[:, 0, 0:256], AF.Copy, [pkey], [("KT", l)])
                checkpoint("blk%d%d" % (l, kv))
                for mt in range(2):
                    S.dma("sp", outd[l, mt * 128:(mt + 1) * 128, :], Ld["stage"][mt][:, :], reads=[("stage", mt)])
                checkpoint("out%d%d" % (l, kv))
        L.free()
        checkpoint("memkv")

        def hgrn_group(h0, tcols, nchunks, seqs, Lh, prefix, first):
            Tp = nchunks * 64
            qT, W1, W2, KBb, TMP, iT, gT, onr, EL = (Lh[k] for k in ("qT", "W1", "W2", "KB", "TMP", "iT", "gT", "onr", "EL"))
            ns = len(seqs)

            def ev(dst, func, nm):
                def f(j, P3, pkey):
                    act(dst[:, j, 0:tcols].rearrange("p (h n) -> p h n", h=2), P3, func, [pkey], [(nm, j)])
                return f
            rmask = Lh["rmask"]

            def proj(nm, dst, func, blk):
                wv, wkey = wload("w_in_a", 0, 0, 16, blk * D + h0 * 128, 256)
                fm_proj(wv, wkey, 16, 2, hn, "hn", tcols, ev(dst, func, nm))

            proj("W1", W1, AF.Sigmoid, 1)
            J = range(2)
            w1 = [W1[:, j, 0:tcols] for j in J]
            w2 = [W2[:, j, 0:tcols] for j in J]
            kb = [KBb[:, j, 0:tcols] for j in J]
            tmp = [TMP[:, j, 0:tcols] for j in J]
            k1 = [("W1", j) for j in J]
            k2 = [("W2", j) for j in J]
            kk = [("KB", j) for j in J]
            kt = [("TMP", j) for j in J]
            ke = [("EL", j) for j in J]
            for j in J:
                tsc(w1[j], w1[j], oml[:, h0 + j:h0 + j + 1], lbv[:, h0 + j:h0 + j + 1], ALU.mult, ALU.add, [k1[j], "oml", "lbv"], [k1[j]])
            for j in J:
                tsc(kb[j], w1[j], -1.0, 1.0, ALU.mult, ALU.add, [k1[j]], [kk[j]])
            for j in J:
                act(w1[j], w1[j], AF.Ln, [k1[j]], [k1[j]])
            for j in J:
                S.op("dve", lambda e, j=j: e.tensor_tensor_scan(out=w2[j], data0=rmask[:, 0:tcols], data1=w1[j], initial=0.0,
                                                                op0=ALU.mult, op1=ALU.add), [k1[j], "rm"], [k2[j]])
            for j in J:
                act(tmp[j], w2[j], AF.Exp, [k2[j]], [kt[j]], scale=-1.0)
            for j in J:
                tt(kb[j], kb[j], tmp[j], ALU.mult, [kk[j], kt[j]], [kk[j]])
            for j in J:
                if Tp:
                    act(EL[:, j, 0:nchunks], w2[j][:, 63:Tp:64], AF.Exp, [k2[j]], [ke[j]])
                if ns:
                    act(EL[:, j, nchunks:nchunks + ns], w2[j][:, Tp + 7:tcols:8], AF.Exp, [k2[j]], [ke[j]])
            for j in J:
                if Tp:
                    tt(tmp[j][:, 0:Tp].rearrange("p (c i) -> p c i", i=64), kb[j][:, 0:Tp].rearrange("p (c i) -> p c i", i=64),
                       EL[:, j, 0:nchunks].unsqueeze(2).broadcast_to([128, nchunks, 64]), ALU.mult, [kk[j], ke[j]], [kt[j]])
                if ns:
                    tt(tmp[j][:, Tp:tcols].rearrange("p (c i) -> p c i", i=8), kb[j][:, Tp:tcols].rearrange("p (c i) -> p c i", i=8),
                       EL[:, j, nchunks:nchunks + ns].unsqueeze(2).broadcast_to([128, ns, 8]), ALU.mult, [kk[j], ke[j]], [kt[j]])
            proj("iT", iT, AF.Copy, 2)
            if not prefix:
                proj("qT", qT, AF.Silu, 0)
                for j in J:
                    act(w1[j], w2[j], AF.Exp, [k2[j]], [k1[j]])
                for j in J:
                    tt(qT[:, j, 0:tcols], qT[:, j, 0:tcols], w1[j], ALU.mult, [("qT", j), k1[j]], [("qT", j)])
                proj("gT", gT, AF.Silu, 3)
            if not prefix:
                checkpoint("hg_chain")
            chunks = [(c * 64, 64, "p", None, c) for c in range(nchunks)]
            for s0 in range(0, ns, 4):
                chunks.append((Tp + s0 * 8, 32, "s", seqs[s0:s0 + 4], nchunks + s0))
            NB = 3
            nck = len(chunks)
            SbL = Lh["SbL"]
            if not prefix:
                act(SbL[0][:], Sst[:, h0:h0 + 2, :], AF.Copy, [("Sst", h0), ("Sst", h0 + 1)], [("SbL", 0)])

            def bufs(ci):
                b2 = ci % NB
                return Lh["Vtok"][b2], Lh["Ktok"][b2], ("Vtok", b2), ("Ktok", b2), b2

            def stT(ci):
                c0, n, kind, sq_, eli = chunks[ci]
                Vtok, Ktok, kV, kK, b2 = bufs(ci)
                for j in range(2):
                    tr(PT[0:n, j * 128:(j + 1) * 128], iT[:, j, c0:c0 + n], identb[:, :], [("iT", j), "identb"], ["PT"], signal=False)
                    tr(PT[0:n, 256 + j * 128:256 + (j + 1) * 128], TMP[:, j, c0:c0 + n], identb[:, :], [("TMP", j), "identb"], ["PT"],
                       signal=(j == 1))
                act(Vtok[0:n, :], PT[0:n, 0:256], AF.Copy, ["PT"], [kV])
                act(Ktok[0:n, :], PT[0:n, 256:512], AF.Copy, ["PT"], [kK])

            def stA(ci):
                c0, n, kind, sq_, eli = chunks[ci]
                b2 = ci % NB
                AT = Lh["AT"][b2]
                for j in range(2):
                    mm(PN[0:n, 0, j * 64:j * 64 + n], KBb[:, j, c0:c0 + n], qT[:, j, c0:c0 + n], True, True,
                       [("KB", j), ("qT", j)], ["PN"], signal=(j == 1))
                msk = mc if kind == "p" else ms
                tt(AT[0:n, :, 0:n], PN[0:n, 0, 0:128].rearrange("p (j t) -> p j t", j=2)[:, :, 0:n],
                   msk[0:n, 0:n].unsqueeze(1).broadcast_to([n, 2, n]), ALU.mult, ["PN", "mc", "ms"], [("AT", b2)])

            def stK(ci):
                c0, n, kind, sq_, eli = chunks[ci]
                if kind != "p":
                    return
                Vtok, Ktok, kV, kK, b2 = bufs(ci)
                PK, kPK = (PG[:, 0:256], "PG") if ci % 2 == 0 else (PB[:, 0, 0:256], "PB")
                for j in range(2):
                    mm(PK[:, j * 128:(j + 1) * 128], Ktok[0:n, j * 128:(j + 1) * 128], Vtok[0:n, j * 128:(j + 1) * 128], True, True,
                       [kK, kV], [kPK], signal=(j == 1))
                nxt = (ci + 1) % 3
                for j in range(2):
                    stt(Sst[:, h0 + j, :], Sst[:, h0 + j, :], EL[:, j, eli:eli + 1], PK[:, j * 128:(j + 1) * 128], ALU.mult, ALU.add,
                        [("Sst", h0 + j), ("EL", j), kPK], [("Sst", h0 + j)])
                if not prefix:
                    act(SbL[nxt][:], Sst[:, h0:h0 + 2, :], AF.Copy, [("Sst", h0), ("Sst", h0 + 1)], [("SbL", nxt)])

            def stO(ci):
                c0, n, kind, sq_, eli = chunks[ci]
                Vtok, Ktok, kV, kK, b2 = bufs(ci)
                AT = Lh["AT"][b2]
                PO, kPO = (PN[:, 1, 0:128], "PN1") if ci % 2 == 0 else (PA[:, 0, 0:128], "PA")
                cur = ci % 3
                if kind == "s":
                    S.op("dve", lambda e: e.memset(PO, 0.0), [], [kPO])
                for j in range(2):
                    mm(PO[:, j * 64:j * 64 + n], Vtok[0:n, j * 128:(j + 1) * 128], AT[0:n, j, 0:n], kind == "p", False,
                       [kV, ("AT", b2)], [kPO], signal=False)
                    if kind == "p":
                        mm(PO[:, j * 64:j * 64 + n], SbL[cur][:, j, :], qT[:, j, c0:c0 + n], False, True,
                           [("SbL", cur), ("qT", j)], [kPO], signal=(j == 1))
                if kind == "s":
                    for qi, s in enumerate(sq_):
                        act(Lh["Ssb"][qi][:], Lh["Ssf"][qi][:], AF.Copy, [("Ssf", qi)], [("Ssb", qi)])
                    for qi, s in enumerate(sq_):
                        b3 = qi
                        Ssf, Ssb = Lh["Ssf"][b3], Lh["Ssb"][b3]
                        for j in range(2):
                            mm(PO[:, j * 64 + qi * 8:j * 64 + qi * 8 + 8], Ssb[:, j, :], qT[:, j, c0 + qi * 8:c0 + qi * 8 + 8], False, True,
                               [("Ssb", b3), ("qT", j)], [kPO], signal=(j == 1))
                        Ktm = Lh["Ktm"]
                        S.op("dve", lambda e, Ktm=Ktm, Ktok=Ktok, n=n, qi=qi: e.tensor_scalar(
                            out=Ktm[0:n, :], in0=Ktok[0:n, :], scalar1=seqm[0:n, qi:qi + 1], scalar2=None, op0=ALU.mult),
                            [kK, "seqm"], ["Ktm"])
                        for j in range(2):
                            mm(PG[:, j * 128:(j + 1) * 128], Ktm[0:n, j * 128:(j + 1) * 128], Vtok[0:n, j * 128:(j + 1) * 128], True, True,
                               ["Ktm", kV], ["PG"], signal=(j == 1))
                        So = Lh["Sout"][qi % 2]
                        for j in range(2):
                            stt(So[:, j, :], Ssf[:, j, :], EL[:, j, eli + qi:eli + qi + 1], PG[:, j * 128:(j + 1) * 128], ALU.mult, ALU.add,
                                [("Ssf", b3), ("EL", j), "PG"], [("Sout", qi % 2)])
                        S.dma("sp", S_s[s, h0:h0 + 2, :, :].rearrange("h k v -> k h v"), So[:], reads=[("Sout", qi % 2)])
                        if ci + 1 < nck and qi < len(chunks[ci + 1][3]):
                            load_state(chunks[ci + 1][3][qi], qi)
                o3 = PO.rearrange("p (j t) -> p j t", j=2)[:, :, 0:n]
                act(W1[:, :, c0:c0 + n], o3, AF.Copy, [kPO], [("W1", 0), ("W1", 1)])

            def load_state(s, qi):
                S.dma("sp", Lh["Ssf"][qi][:], sh[s, h0:h0 + 2, :, :].rearrange("h k v -> k h v"), writes=[("Ssf", qi)])

            if not prefix and nck > nchunks:
                for qi, s in enumerate(chunks[nchunks][3]):
                    load_state(s, qi)
            for c_ in range(min(2, nck)):
                stT(c_)
                if not prefix:
                    stA(c_)
            stK(0)
            for ci in range(nck):
                if ci + 2 < nck:
                    stT(ci + 2)
                    if not prefix:
                        stA(ci + 2)
                if ci + 1 < nck:
                    stK(ci + 1)
                if not prefix:
                    stO(ci)
            if not prefix:
                nh = tcols // 2
                for j in range(2):
                    act(TMP[:, j, 0:tcols], W1[:, j, 0:tcols], AF.Square, [("W1", j)], [("TMP", j)])
                    PX, pkey = next_px()
                    for h in range(2):
                        mm(PX[:, h, 0:nh], onesb[:], TMP[:, j, h * nh:(h + 1) * nh], True, True, [("TMP", j), "onesb"], [pkey],
                           signal=(h == 1))
                    act(W2[:, j, 0:tcols].rearrange("p (h n) -> p h n", h=2), PX[:, :, 0:nh], AF.Ln, [pkey, "epst"], [("W2", j)],
                        scale=1.0 / 128, bias=epst[:, 0:1])
                    act(W2[:, j, 0:tcols], W2[:, j, 0:tcols], AF.Exp, [("W2", j)], [("W2", j)], scale=-0.5)
                    tt(W1[:, j, 0:tcols], W1[:, j, 0:tcols], W2[:, j, 0:tcols], ALU.mult, [("W1", j), ("W2", j)], [("W1", j)])
                    stt(onr[:, j, 0:tcols], W1[:, j, 0:tcols], hgn[:, 0:1], gT[:, j, 0:tcols], ALU.mult, ALU.mult,
                        [("W1", j), "hgn", ("gT", j)], [("onr", j)])
            if not prefix:
                checkpoint("hg_chunks")
                out_proj(onr, "onr", 2, "w_out_a", 0, h0 * 128, first)
                checkpoint("hg_out")

        def hgrn_locals(Lx):
            d = {}
            for nm, dt in (("qT", BF16), ("W1", F32), ("W2", F32), ("KB", BF16), ("TMP", BF16), ("iT", BF16), ("gT", BF16), ("onr", BF16)):
                d[nm] = Lx.sb("h_" + nm, [128, 2, T], dt)
            d["EL"] = Lx.sb("h_EL", [128, 2, 24])
            d["Vtok"] = [Lx.sb("h_Vtok%d" % i, [64, 256], BF16) for i in range(3)]
            d["Ktok"] = [Lx.sb("h_Ktok%d" % i, [64, 256], BF16) for i in range(3)]
            d["Ktm"] = Lx.sb("h_Ktm", [64, 256], BF16)
            d["AT"] = [Lx.sb("h_AT%d" % i, [64, 2, 64], BF16) for i in range(3)]
            d["SbL"] = [Lx.sb("h_SbL%d" % i, [128, 2, 128], BF16) for i in range(3)]
            d["Ssf"] = [Lx.sb("h_Ssf%d" % i, [128, 2, 128]) for i in range(4)]
            d["Ssb"] = [Lx.sb("h_Ssb%d" % i, [128, 2, 128], BF16) for i in range(4)]
            d["Sout"] = [Lx.sb("h_Sout%d" % i, [128, 2, 128]) for i in range(2)]
            d["sloaded"] = {}
            return d

        for sbk in range(NPRE // TPRE):
            L = Local()
            Ld = dict(stage=[L.sb("stage0", [128, D]), L.sb("stage1", [128, D])])
            load_x(xpre[sbk * TPRE:(sbk + 1) * TPRE, :], TPRE, 0, Ld)
            prenorm(V_MIXPRE + 0, TPRE)
            L.free()
            L = Local()
            Lh = hgrn_locals(L)
            Lh["rmask"] = rm
            S.dma("sp", rm[:, 0:TPRE], rmpre_d[:, :], writes=["rm"])
            for g in range(8):
                hgrn_group(2 * g, TPRE, TPRE // 64, [], Lh, True, False)
            L.free()

        checkpoint("prefix")

        def attn_layer(l, p):
            nch, seqs = p["nch"], p["seqs"]
            Tp = nch * 64
            nhp = Tp // 2
            ns = len(seqs)
            L = Local()
            qh = L.sb("a_qh", [128, 4, T], BF16)
            ah = L.sb("a_ah", [128, 4, T], BF16)
            ES = L.sb("a_ES", [128, 2, 576], BF16)
            rinv = L.sb("a_rinv", [128, 576])
            Ks = [L.sb("a_Ks%d" % i, [128, 2, 512], BF16) for i in range(2)]
            Vs = [L.sb("a_Vs%d" % i, [128, 2, 512], BF16) for i in range(3)]
            KTs = [L.sb("a_KTs%d" % i, [128, 4, 256], BF16) for i in range(2)]
            ESs = [L.sb("a_ESs%d" % i, [128, 2, 8], BF16) for i in range(2)]
            rinvs = L.sb("a_rinvs", [128, 8])
            sc = 512.0 ** -0.5
            for hd in range(4):
                for b in range(2):
                    wv, wkey = wload("w_xq", l, 0, 16, hd * 512 + b * 256, 256)

                    def evq(j, P3, pkey, b=b):
                        act(qh[:, 2 * b + j, :].rearrange("p (h n) -> p h n", h=2), P3, AF.Copy, [pkey], [("qh", 2 * b + j)])
                    fm_proj(wv, wkey, 16, 2, hn, "hn", T, evq)
                for mt in range(2):
                    PX, pkey = next_px()
                    for dc in range(4):
                        for h in range(2):
                            mm(PX[:, h, 0:nhp], KT[l][:, hd * 4 + dc, mt * 128:(mt + 1) * 128], qh[:, dc, h * nhp:(h + 1) * nhp],
                               dc == 0, dc == 3, [("KT", l), ("qh", dc)], [pkey], signal=(dc == 3 and h == 1))
                    act(ES[:, mt, 0:Tp].rearrange("p (h n) -> p h n", h=2), PX[:, :, 0:nhp], AF.Exp, [pkey], [("ES", mt)], scale=sc)
                for mt in range(2):
                    for h in range(2):
                        mm(PN[:, h, 0:nhp], onesb[:], ES[:, mt, h * nhp:(h + 1) * nhp], mt == 0, mt == 1, [("ES", mt), "onesb"], ["PN"],
                           signal=(mt == 1 and h == 1))
                S.op("dve", lambda e: e.reciprocal(out=rinv[:, 0:Tp].rearrange("p (h n) -> p h n", h=2), in_=PN[:, :, 0:nhp]),
                     ["PN"], ["rinv"])
                for dc in range(4):
                    PX, pkey = next_px()
                    for mt in range(2):
                        for h in range(2):
                            mm(PX[:, h, 0:nhp], VV[l][:, mt, hd * 512 + dc * 128: hd * 512 + (dc + 1) * 128], ES[:, mt, h * nhp:(h + 1) * nhp],
                               mt == 0, mt == 1, [("VV", l), ("ES", mt)], [pkey], signal=(mt == 1 and h == 1))
                    tt(ah[:, dc, 0:Tp].rearrange("p (h n) -> p h n", h=2), PX[:, :, 0:nhp],
                       rinv[:, 0:Tp].rearrange("p (h n) -> p h n", h=2), ALU.mult, [pkey, "rinv"], [("ah", dc)])
                def stA1(si, hd=hd):
                    s = seqs[si]
                    b2 = si % 2
                    b3 = si % 3
                    S.dma("pool", Ks[b2][:], ck[l, s, :, hd * 512:(hd + 1) * 512].rearrange("(t p) d -> p t d", p=128), writes=[("Ks", b2)])
                    S.dma("pool", Vs[b3][:], cv[l, s, :, hd * 512:(hd + 1) * 512].rearrange("(t p) d -> p t d", p=128), writes=[("Vs", b3)])
                    for dc in range(4):
                        for mt in range(2):
                            tr(PT[:, dc * 256 + mt * 128: dc * 256 + (mt + 1) * 128], Ks[b2][:, mt, dc * 128:(dc + 1) * 128], identb[:, :],
                               [("Ks", b2), "identb"], ["PT"], signal=(dc == 3 and mt == 1))
                    act(KTs[b2][:].rearrange("p c m -> p (c m)"), PT[:, :], AF.Copy, ["PT"], [("KTs", b2)])

                def stA2(si, hd=hd):
                    b2 = si % 2
                    c0 = Tp + si * 8
                    for mt in range(2):
                        for dc in range(4):
                            mm(PG[:, mt * 8:(mt + 1) * 8], KTs[b2][:, dc, mt * 128:(mt + 1) * 128], qh[:, dc, c0:c0 + 8], dc == 0, dc == 3,
                               [("KTs", b2), ("qh", dc)], ["PG"], signal=(dc == 3 and mt == 1))
                    act(ESs[b2][:].rearrange("p m t -> p (m t)"), PG[:, 0:16], AF.Exp, ["PG"], [("ESs", b2)], scale=sc)

                def stB(si, hd=hd):
                    b2 = si % 2
                    b3 = si % 3
                    c0 = Tp + si * 8
                    for mt in range(2):
                        mm(PN[:, 0, 0:8], onesb[:], ESs[b2][:, mt, :], mt == 0, mt == 1, [("ESs", b2), "onesb"], ["PN"], signal=(mt == 1))
                    S.op("dve", lambda e: e.reciprocal(out=rinvs[:], in_=PN[:, 0, 0:8]), ["PN"], ["rinvs"])
                    for dc in range(4):
                        for mt in range(2):
                            mm(PN[:, 1, dc * 8:(dc + 1) * 8], Vs[b3][:, mt, dc * 128:(dc + 1) * 128], ESs[b2][:, mt, :], mt == 0, mt == 1,
                               [("Vs", b3), ("ESs", b2)], ["PN"], signal=(dc == 3 and mt == 1))
                    tt(ah[:, :, c0:c0 + 8], PN[:, 1, 0:32].rearrange("p (c t) -> p c t", t=8),
                       rinvs[:].unsqueeze(1).broadcast_to([128, 4, 8]), ALU.mult, ["PN", "rinvs"], [("ah", dc_) for dc_ in range(4)])

                stA1(0)
                stA2(0)
                for si in range(ns):
                    if si + 1 < ns:
                        stA1(si + 1)
                    stB(si)
                    if si + 1 < ns:
                        stA2(si + 1)
                out_proj(ah, "ah", 4, "w_xo", l, hd * 512, hd == 0)
            L.free()

        def mlp_layer(l):
            L = Local()
            hT = L.sb("m_hT", [128, 8, T], BF16)
            rr = [L.sb("m_r%d" % i, [128, T], BF16) for i in range(2)]
            for g in range(8):
                for b in range(4):
                    wv, wkey = wload("w_up", l, 0, 16, g * 1024 + b * 256, 256)

                    def evu(j, P3, pkey, b=b):
                        i2 = st.get("rr", 0) % 2
                        st["rr"] = st.get("rr", 0) + 1
                        r3 = rr[i2][:, :].rearrange("p (h n) -> p h n", h=2)
                        act(r3, P3, AF.Relu, [pkey], [("rr", i2)])
                        tt(hT[:, 2 * b + j, :], rr[i2][:, :], rr[i2][:, :], ALU.mult, [("rr", i2)], [("hT", 2 * b + j)])
                    fm_proj(wv, wkey, 16, 2, hn, "hn", T, evu)
                out_proj(hT, "hT", 8, "w_down", l, g * 1024, g == 0)
            L.free()

        def pool_layer(p, pi):
            nch, seqs = p["nch"], p["seqs"]
            Tp = nch * 64
            ns = len(seqs)
            Ts = ns * 8
            LE = 15 + Tp + 23 * ns
            L = Local()
            Ec = [L.sb("p_Ec%d" % i, [128, 803]) for i in range(2)]
            Wa = L.sb("p_Wa", [128, 803])
            Wb = L.sb("p_Wb", [128, 803])
            pl = L.sb("p_pl", [128, 4, T], BF16)
            mx = L.sb("p_mx", [128, 4, T], BF16)
            invc = L.sb("p_invc", [128, T])
            ust = L.sb("p_ust", [128, 4, 96])
            oldst = L.sb("p_oldst", [128, 512])
            Ld = dict(stage=[L.sb("p_stg0", [128, 512]), L.sb("p_stg1", [128, 512])])
            for gi in range(4):
                w = 2 << gi
                S.dma("sp", invc[:], invc_d[pi, gi, :, :], writes=["invc"])
                for b in range(2):
                    wv, wkey = wload("w_in_b", 0, 0, 16, gi * 512 + b * 256, 256)

                    def evp(j, P3, pkey, b=b, gi=gi, w=w):
                        cg = 2 * b + j
                        c = gi * 4 + cg
                        i2 = st.get("ec", 0) % 2
                        st["ec"] = st.get("ec", 0) + 1
                        E = Ec[i2]
                        ke = ("Ec", i2)
                        act(Wb[:, 0:T].rearrange("p (h n) -> p h n", h=2), P3, AF.Copy, [pkey], ["Wb"])
                        S.op("dve", lambda e: e.tensor_copy(out=E[:, 15:15 + Tp], in_=Wb[:, 0:Tp]), ["Wb"], [ke])
                        S.op("dve", lambda e: e.tensor_copy(out=E[:, 15 + Tp:LE].rearrange("p (s r) -> p s r", r=23)[:, :, 15:23],
                                                            in_=Wb[:, Tp:T].rearrange("p (s r) -> p s r", r=8)), ["Wb"], [ke])
                        S.op("dve", lambda e: e.tensor_copy(out=ust[:, cg, 0:Ts], in_=Wb[:, Tp:T]), ["Wb"], [("ust", cg)])
                        act(E[:, 0:15], tailbuf[:, c, :], AF.Copy, [("tail", c)], [ke])
                        if pi == 0:
                            S.op("dve", lambda e: e.tensor_scalar(out=E[:, 15:79], in0=E[:, 15:79], scalar1=flag[:, 0:1],
                                                                  scalar2=None, op0=ALU.mult), [ke, "flag"], [ke])
                        act(tailbuf[:, c, :], E[:, Tp:Tp + 15], AF.Copy, [ke], [("tail", c)])
                        for s0 in range(0, ns, 8):
                            n8 = min(8, ns - s0)
                            if cg == 0:
                                pass
                            PXo, pko = next_px()
                            S.dma("sp", oldst[0:n8 * 15, 0:128],
                                  spool[seqs[s0]:seqs[s0] + n8, :, c * 128:(c + 1) * 128].rearrange("s r d -> (s r) d"),
                                  writes=["oldst"])
                            tr(PXo[:, 0, 0:n8 * 15], oldst[0:n8 * 15, 0:128], identf[0:n8 * 15, 0:n8 * 15], ["oldst", "identf"], [pko])
                            act(E[:, 15 + Tp + 23 * s0:15 + Tp + 23 * (s0 + n8)].rearrange("p (s r) -> p s r", r=23)[:, :, 0:15],
                                PXo[:, 0, 0:n8 * 15].rearrange("p (s r) -> p s r", r=15), AF.Copy, [pko], [ke])
                        cur = E
                        ck_ = ke
                        bufs = [(Wa, "Wa"), (Wb, "Wb")]
                        lo = 0
                        for k in range(gi + 1):
                            sh_ = 1 << k
                            nb, nk_ = bufs[k % 2]
                            lo2 = lo + sh_
                            tt(nb[:, lo2:LE], cur[:, lo2:LE], cur[:, lo2 - sh_:LE - sh_], ALU.add, [ck_], [nk_])
                            cur, ck_, lo = nb, nk_, lo2
                        oth, ko = bufs[(gi + 1) % 2]
                        tt(oth[:, 0:Tp], cur[:, 15:15 + Tp], invc[:, 0:Tp], ALU.mult, [ck_, "invc"], [ko])
                        tt(pl[:, cg, 0:Tp], oth[:, 0:Tp], E[:, 15:15 + Tp], ALU.subtract, [ko, ke], [("pl", cg)])
                        cs3 = cur[:, 15 + Tp:LE].rearrange("p (s r) -> p s r", r=23)[:, :, 15:23]
                        es3 = E[:, 15 + Tp:LE].rearrange("p (s r) -> p s r", r=23)[:, :, 15:23]
                        o3 = oth[:, Tp:T].rearrange("p (s r) -> p s r", r=8)
                        tt(o3, cs3, invc[:, Tp:T].rearrange("p (s r) -> p s r", r=8), ALU.mult, [ck_, "invc"], [ko])
                        tt(pl[:, cg, Tp:T].rearrange("p (s r) -> p s r", r=8), o3, es3, ALU.subtract, [ko, ke], [("pl", cg)])
                    fm_proj(wv, wkey, 16, 2, hn, "hn", T, evp)
                for s0 in range(0, ns, 12):
                    n12 = min(12, ns - s0)
                    PXo, pko = next_px()
                    for cg in range(4):
                        tr(PXo[0:n12 * 8, cg // 2, (cg % 2) * 128:(cg % 2) * 128 + 128], ust[:, cg, s0 * 8:(s0 + n12) * 8], identf[:, :],
                           [("ust", cg), "identf"], [pko], signal=(cg == 3))
                    stg = Ld["stage"][gi % 2]
                    sk = ("pstg", gi % 2)
                    act(stg[0:n12 * 8, :].rearrange("p (h n) -> p h n", h=2), PXo[0:n12 * 8, :, 0:256], AF.Copy, [pko], [sk])
                    for si in range(n12):
                        s = seqs[s0 + si]
                        S.dma("sp", pool_s[s, 7:15, gi * 512:(gi + 1) * 512], stg[si * 8:(si + 1) * 8, :], reads=[sk])
                for b in range(2):
                    wv, wkey = wload("pool_w", gi, 0, 4, b * 256, 256)

                    def evm(j, P3, pkey, b=b, gi=gi):
                        cg = 2 * b + j
                        act(mx[:, cg, :].rearrange("p (h n) -> p h n", h=2), P3, AF.Copy, [pkey], [("mx", cg)],
                            scale=vec[:, V_PSCALE, gi * 4 + cg:gi * 4 + cg + 1])
                    fm_proj(wv, wkey, 4, 2, pl, "pl", T, evm)
                out_proj(mx, "mx", 4, "w_out_b", 0, gi * 512, gi == 0)
            L.free()

        S.dma("sp", pool_s[:, 0:7, :], spool[:, 8:15, :])

        for pi, p in enumerate(PASSES):
            nch, seqs = p["nch"], p["seqs"]
            Tp = nch * 64
            ns = len(seqs)
            L = Local()
            Ld = dict(stage=[L.sb("stage0", [128, D]), L.sb("stage1", [128, D])])
            load_x(xp[p["row0"]:p["row0"] + Tp, :], Tp, 0, Ld)
            load_x(xs[seqs[0] * 8:(seqs[0] + ns) * 8, :], ns * 8, Tp, Ld)
            S.dma("sp", rm[:, :], rm_d[pi, :, :], writes=["rm"])
            L.free()
            checkpoint("p%dload" % pi)
            for l in range(2):
                prenorm(V_MIXPRE + l, T)
                if l == 0:
                    L = Local()
                    Lh = hgrn_locals(L)
                    Lh["rmask"] = rm
                    for g in range(8):
                        hgrn_group(2 * g, T, nch, seqs, Lh, False, g == 0)
                    L.free()
                else:
                    pool_layer(p, pi)
                if pi == 0 and l == 0:
                    dbg_dump("d_hg", Fb[:, :, Tp:Tp + 32], [128, 16, 32], [("F", c_) for c_ in range(16)])
                checkpoint("p%dl%dmix0" % (pi, l))
                postnorm(V_MIXPOST + l)
                checkpoint("p%dl%dmix" % (pi, l))
                prenorm(V_XPRE + l, T)
                attn_layer(l, p)
                if pi == 0 and l == 0:
                    dbg_dump("d_at", Fb[:, :, Tp:Tp + 32], [128, 16, 32], [("F", c_) for c_ in range(16)])
                postnorm(V_XPOST + l)
                checkpoint("p%dl%dattn" % (pi, l))
                prenorm(V_MLPPRE + l, T)
                mlp_layer(l)
                postnorm(V_MLPPOST + l)
                checkpoint("p%dl%dmlp" % (pi, l))
            L = Local()
            Ld = dict(stage=[L.sb("stage0", [128, D]), L.sb("stage1", [128, D])])
            if pi == 0:
                store_rows(xT, "x", 64, Tp - 64, y_p[0:Tp - 64, :], Ld)
            else:
                store_rows(xT, "x", 0, Tp, y_p[512:1024, :], Ld)
            store_rows(xT, "x", Tp, ns * 8, y_s[seqs[0] * 8:(seqs[0] + ns) * 8, :], Ld)
            L.free()

        for h in range(16):
            S.dma("sp", S_p[h, :, :], Sst[:, h, :], reads=[("Sst", h)])
        L = Local()
        Ld = dict(stage=[L.sb("stage0", [128, D]), L.sb("stage1", [128, D])])
        store_rows(tailbuf, "tail", 0, 15, pool_p[:, :], Ld)
        L.free()
    try:
        body()
    except _Stop:
        for Lx in list(reversed(live)):
            Lx.free()
    S.barrier(final=True)
    S.close()
    for cm in reversed(cms):
        cm.__exit__(None, None, None)
    S.specs = specs
    return nc, S


_CACHE = {}


def kernel(x_prompt, x_sample, state_hgrn, state_pool, cache_mem_k, cache_mem_v, mem_prompt,
           w_in_a, hg_lb_logits, hg_norm, w_out_a, w_in_b, pool_w, pool_scale, w_out_b,
           norm_mem, w_xq, w_xkv, w_xo, norm_mix_pre, norm_mix_post, norm_x_pre, norm_x_post,
           norm_mlp_pre, norm_mlp_post, w_up, w_down):
    f = np.float32
    A = lambda a: np.ascontiguousarray(np.asarray(a, dtype=f))
    x_prompt, x_sample = A(x_prompt), A(x_sample)
    if "nc" not in _CACHE:
        _CACHE["nc"], S_ = build_program()
        _CACHE["specs"] = S_.specs
    nc = _CACHE["nc"]
    vecs = [norm_mix_pre, norm_mix_post, norm_x_pre, norm_x_post, norm_mlp_pre, norm_mlp_post, norm_mem]
    rows = []
    for v in vecs:
        v = A(v)
        rows += [v[0], v[1]]
    rows.append(A(pool_scale)[0])
    vec = np.ascontiguousarray(np.stack(rows, 0).reshape(15, 16, 128).transpose(2, 0, 1))
    lbl = np.ascontiguousarray(A(hg_lb_logits).reshape(3, 16, 128).transpose(2, 0, 1))
    hgn = np.ascontiguousarray(A(hg_norm)[0].reshape(128, 1))
    ident = np.eye(128, dtype=f)
    mc = np.triu(np.ones((64, 64), f))
    sid = np.arange(32) // 8
    ms = (np.triu(np.ones((32, 32), f)) * (sid[:, None] == sid[None, :])).astype(f)
    seqm = (sid[:, None] == np.arange(4)[None, :]).astype(f)
    rm = np.ones((2, 128, T), f)
    invc = np.ones((2, 4, 128, T), f)
    rmpre = np.ones((128, TPRE), f)
    rmpre[:, 0::64] = 0.0
    wsrc = dict(w_in_a=A(w_in_a), w_out_a=A(w_out_a), w_in_b=A(w_in_b), pool_w=A(pool_w)[0],
                w_out_b=A(w_out_b), w_xq=A(w_xq), w_xkv=A(w_xkv), w_xo=A(w_xo), w_up=A(w_up), w_down=A(w_down))
    specs = _CACHE["specs"]
    assert len(specs) <= NBLK, len(specs)
    wblk = np.zeros((NBLK, 128, 4096), f)
    for (name, idx, r0, nk, c0, C), bid in specs.items():
        blk = wsrc[name][idx][r0:r0 + nk * 128, c0:c0 + C]
        wblk[bid, :, 0:nk * C] = blk.reshape(nk, 128, C).transpose(1, 0, 2).reshape(128, nk * C)
    weights = dict(wblk=wblk)
    state_hgrn, state_pool = A(state_hgrn), A(state_pool)
    cache_mem_k, cache_mem_v, mem_prompt = A(cache_mem_k), A(cache_mem_v), A(mem_prompt)
    in_maps = []
    for c in range(8):
        b, half = c // 2, c % 2
        rmc = rm.copy()
        ivc = invc.copy()
        for pi, p in enumerate(PASSES):
            Tp = p["nch"] * 64
            rmc[pi, :, 0:Tp:64] = 0.0
            rmc[pi, :, Tp::8] = 0.0
            pos = half * 1024 - 64 + p["row0"] + np.arange(Tp)
            for gi in range(4):
                w = 2 << gi
                ivc[pi, gi, :, 0:Tp] = (1.0 / np.minimum(w, np.maximum(pos, 0) + 1))[None, :]
                ivc[pi, gi, :, Tp:] = 1.0 / w
        xpc = np.zeros((1088, 2048), f)
        lo = half * 1024 - 64
        if lo < 0:
            xpc[64:] = x_prompt[b, 0:1024]
        else:
            xpc[:] = x_prompt[b, lo:lo + 1088]
        xpre = np.zeros((NPRE, 2048), f)
        if half == 1:
            xpre[:] = x_prompt[b, 0:NPRE]
        s0 = c * 16
        m = dict(
            xp=xpc, xpre=xpre, xs=np.ascontiguousarray(x_sample[s0:s0 + 16].reshape(128, 2048)),
            sh=np.ascontiguousarray(state_hgrn[0, s0:s0 + 16]), spool=np.ascontiguousarray(state_pool[0, s0:s0 + 16]),
            ck=np.ascontiguousarray(cache_mem_k[:, s0:s0 + 16].reshape(2, 16, 256, 2048)),
            cv=np.ascontiguousarray(cache_mem_v[:, s0:s0 + 16].reshape(2, 16, 256, 2048)),
            mem=np.ascontiguousarray(mem_prompt[b]), vec=vec, lbl=lbl, hgn=hgn,
            flag=np.full((128, 1), float(half), f), ident=ident, mc=mc, ms=ms, seqm=seqm, rm=rmc, rmpre=rmpre, invc=ivc,
        )
        m.update(weights)
        in_maps.append(m)
    if _CACHE.get("dbg_only_maps"):
        return in_maps
    res = run_bass_kernel_spmd(nc, in_maps, core_ids=list(range(8)))
    R = res.results
    y_prompt = np.stack([np.concatenate([R[2 * b]["y_p"], R[2 * b + 1]["y_p"]], 0) for b in range(4)], 0)
    y_sample = np.concatenate([R[c]["y_s"] for c in range(8)], 0).reshape(128, 8, 2048)
    st_h_p = np.stack([R[2 * b + 1]["S_p"] for b in range(4)], 0)[None]
    st_p_p = np.stack([R[2 * b + 1]["pool_p"] for b in range(4)], 0)[None]
    mk = np.stack([R[2 * b]["mk_o"] for b in range(4)], 1).reshape(2, 4, 256, 4, 512)
    mv = np.stack([R[2 * b]["mv_o"] for b in range(4)], 1).reshape(2, 4, 256, 4, 512)
    st_h_s = np.concatenate([R[c]["S_s"] for c in range(8)], 0)[None]
    st_p_s = np.concatenate([R[c]["pool_s"] for c in range(8)], 0)[None]
    return (y_prompt.astype(f), y_sample.astype(f), st_h_p.astype(f), st_p_p.astype(f), mk.astype(f), mv.astype(f),
            st_h_s.astype(f), st_p_s.astype(f))
```

```python
import os
import numpy as np
import concourse.bass as bass
import concourse.mybir as mybir
from concourse.bass_utils import run_bass_kernel_spmd

F32 = mybir.dt.float32
BF16 = mybir.dt.bfloat16
AF = mybir.ActivationFunctionType
ALU = mybir.AluOpType

D = 2048
NCH = 16
T = 608
EPS = 1e-6
PASSES = [dict(nch=9, seqs=list(range(0, 4)), row0=0), dict(nch=8, seqs=list(range(4, 16)), row0=576)]
NPRE = 960
NBLK = 256
TPRE = 320
V_MIXPRE, V_MIXPOST, V_XPRE, V_XPOST, V_MLPPRE, V_MLPPOST, V_MEM, V_PSCALE = 0, 2, 4, 6, 8, 10, 12, 14


class Sync:
    def __init__(self, nc, ndma_sems=12):
        self.nc = nc
        self.eng = {"pe": nc.tensor, "act": nc.scalar, "dve": nc.vector, "pool": nc.gpsimd, "sp": nc.sync}
        self._cms = []
        self.sem = {}
        for e in self.eng:
            cm = nc.semaphore("sem_" + e)
            self.sem[e] = cm.__enter__()
            self._cms.append(cm)
        self.cnt = {e: 0 for e in self.eng}
        self.known = {e: {} for e in self.eng}
        self.dsem = {}
        self.dval = {}
        self.drot = {}
        for q in ("sp", "pool", "act"):
            lst = []
            for i in range(ndma_sems):
                cm = nc.semaphore("dsem_%s_%d" % (q, i))
                lst.append(cm.__enter__())
                self._cms.append(cm)
            self.dsem[q] = lst
            self.dval[q] = [0] * ndma_sems
            self.drot[q] = 0
        self.last_w = {}
        self.readers = {}
        self.ninstr = 0

    def close(self):
        for cm in reversed(self._cms):
            cm.__exit__(None, None, None)

    def _wait(self, e, tok):
        if tok[0] == "e":
            _, pe, n = tok
            if pe == e and e == "pe":
                return
            key = pe
            semh = self.sem[pe]
            val = n
        else:
            _, q, idx, val = tok
            key = (q, idx)
            semh = self.dsem[q][idx]
        if self.known[e].get(key, 0) >= val:
            return
        self.eng[e].wait_ge(semh, val)
        self.known[e][key] = val

    def _deps(self, e, reads, writes):
        toks = []
        for r in reads:
            t = self.last_w.get(r)
            if t is not None:
                toks.append(t)
        for w in writes:
            t = self.last_w.get(w)
            if t is not None:
                toks.append(t)
            toks.extend(self.readers.get(w, ()))
        for t in toks:
            self._wait(e, t)

    def _record(self, tok, reads, writes):
        for r in reads:
            lst = self.readers.setdefault(r, [])
            if tok not in lst:
                lst.append(tok)
        for w in writes:
            self.last_w[w] = tok
            self.readers[w] = []

    def op(self, e, fn, reads=(), writes=(), signal=True):
        self._deps(e, reads, writes)
        ins = fn(self.eng[e])
        self.ninstr += 1
        if signal:
            self.cnt[e] += 1
            ins.then_inc(self.sem[e], 1)
            tok = ("e", e, self.cnt[e])
        else:
            tok = ("e", e, self.cnt[e] + 1)
        self._record(tok, reads, writes)
        return ins

    def dma(self, q, out, in_, reads=(), writes=(), **kw):
        self._deps(q, reads, writes)
        idx = self.drot[q]
        self.drot[q] = (idx + 1) % len(self.dsem[q])
        if self.dval[q][idx] > 0:
            self._wait(q, ("d", q, idx, self.dval[q][idx]))
        ins = self.eng[q].dma_start(out=out, in_=in_, **kw)
        self.ninstr += 1
        self.dval[q][idx] += 16
        ins.then_inc(self.dsem[q][idx], 16)
        tok = ("d", q, idx, self.dval[q][idx])
        self._record(tok, reads, writes)
        return tok

    def barrier(self, final=False):
        for e in self.eng:
            if e == "pe" and not final:
                continue
            for q in self.dsem:
                for idx, v in enumerate(self.dval[q]):
                    if v > 0:
                        self._wait(e, ("d", q, idx, v))
            for pe in self.eng:
                if pe != e and self.cnt[pe] > 0:
                    self._wait(e, ("e", pe, self.cnt[pe]))
        for k in list(self.readers):
            if not (isinstance(k, str) and k.startswith("P")):
                self.readers[k] = []


class _Stop(Exception):
    pass


def build_program(stop=None):
    nc = bass.Bass("TRN2", target_bir_lowering=False)
    live = []

    def checkpoint(name):
        if stop == name:
            raise _Stop()

    def din(name, shape):
        return nc.dram_tensor(name, list(shape), F32, kind="ExternalInput").ap()

    def dout(name, shape):
        return nc.dram_tensor(name, list(shape), F32, kind="ExternalOutput").ap()

    xp = din("xp", [1088, D])
    xpre = din("xpre", [NPRE, D])
    xs = din("xs", [128, D])
    sh = din("sh", [16, 16, 128, 128])
    spool = din("spool", [16, 15, D])
    ck = din("ck", [2, 16, 256, D])
    cv = din("cv", [2, 16, 256, D])
    mem = din("mem", [256, D])
    wblk = din("wblk", [NBLK, 128, 4096])
    vec_d = din("vec", [128, 15, 16])
    lbl_d = din("lbl", [128, 3, 16])
    hgn_d = din("hgn", [128, 1])
    flag_d = din("flag", [128, 1])
    ident_d = din("ident", [128, 128])
    mc_d = din("mc", [64, 64])
    ms_d = din("ms", [32, 32])
    seqm_d = din("seqm", [32, 4])
    rm_d = din("rm", [2, 128, T])
    rmpre_d = din("rmpre", [128, TPRE])
    invc_d = din("invc", [2, 4, 128, T])

    y_p = dout("y_p", [1024, D])
    y_s = dout("y_s", [128, D])
    S_p = dout("S_p", [16, 128, 128])
    pool_p = dout("pool_p", [15, D])
    mk_o = dout("mk_o", [2, 256, D])
    mv_o = dout("mv_o", [2, 256, D])
    S_s = dout("S_s", [16, 16, 128, 128])
    pool_s = dout("pool_s", [16, 15, D])

    S = Sync(nc)
    cms = []

    def sb(name, shape, dt=F32):
        cm = nc.sbuf_tensor("s_" + name, list(shape), dt)
        cms.append(cm)
        return cm.__enter__()

    def psum(name, shape, dt=F32):
        cm = nc.psum_tensor("p_" + name, list(shape), dt)
        cms.append(cm)
        return cm.__enter__()

    class Local:
        def __init__(self):
            self.l = []
            live.append(self)

        def sb(self, name, shape, dt=F32):
            st["uid"] = st.get("uid", 0) + 1
            cm = nc.sbuf_tensor("l%d_%s" % (st["uid"], name), list(shape), dt)
            self.l.append(cm)
            return cm.__enter__()

        def free(self):
            S.barrier()
            for cm in reversed(self.l):
                cm.__exit__(None, None, None)
            self.l = []
            live.remove(self)

    xT = sb("xT", [128, NCH, T])
    hn = sb("hn", [128, NCH, T], BF16)
    Fb = sb("Fb", [128, NCH, T])
    ring = [sb("ring%d" % i, [128, 4096], BF16) for i in range(3)]
    KT = [sb("KT%d" % l, [128, 16, 256], BF16) for l in range(2)]
    VV = [sb("VV%d" % l, [128, 2, D], BF16) for l in range(2)]
    Sst = sb("Sst", [128, 16, 128])
    vec = sb("vec", [128, 15, 16])
    lbl = sb("lbl", [128, 3, 16])
    lbv = sb("lbv", [128, 16])
    oml = sb("oml", [128, 16])
    hgn = sb("hgn", [128, 1])
    flag = sb("flag", [128, 1])
    identf = sb("identf", [128, 128])
    identb = sb("identb", [128, 128], BF16)
    onesb = sb("onesb", [128, 128], BF16)
    epst = sb("epst", [128, 1])
    mc = sb("mc", [64, 64])
    ms = sb("ms", [32, 32])
    seqm = sb("seqm", [32, 4])
    rm = sb("rm", [128, T])
    rstd = sb("rstd", [128, T])
    sq = [sb("sq%d" % i, [128, T], BF16) for i in range(2)]
    tailbuf = sb("tailbuf", [128, NCH, 15])

    PA = psum("PA", [128, 2, 512])
    PB = psum("PB", [128, 2, 512])
    PN = psum("PN", [128, 2, 512])
    PG = psum("PG", [128, 512])
    PT = psum("PT", [128, 1024], BF16)
    st = dict(ring=0, px=0, sq=0)

    def act(out, in_, func, reads, writes, **kw):
        S.op("act", lambda e: e.activation(out=out, in_=in_, func=func, **kw), reads, writes)

    def tt(out, in0, in1, op, reads, writes):
        S.op("dve", lambda e: e.tensor_tensor(out=out, in0=in0, in1=in1, op=op), reads, writes)

    def tsc(out, in0, s1, s2, op0, op1, reads, writes):
        S.op("dve", lambda e: e.tensor_scalar(out=out, in0=in0, scalar1=s1, scalar2=s2, op0=op0, op1=op1), reads, writes)

    def stt(out, in0, scalar, in1, op0, op1, reads, writes):
        S.op("dve", lambda e: e.scalar_tensor_tensor(out=out, in0=in0, scalar=scalar, in1=in1, op0=op0, op1=op1),
             reads, writes)

    def mm(out, lhsT, rhs, start, stop, reads, writes, signal):
        S.op("pe", lambda e: e.matmul(out, lhsT=lhsT, rhs=rhs, start=start, stop=stop, skip_group_check=True),
             reads, writes, signal=signal)

    def tr(out, in_, ident, reads, writes, signal=True):
        S.op("pe", lambda e: e.transpose(out=out, in_=in_, identity=ident), reads, writes, signal=signal)

    def halves(ap, tt_):
        return ap.rearrange("p (h n) -> p h n", h=2)

    specs = {}

    def wload(name, idx, r0, nk, c0, C):
        spec = (name, idx, r0, nk, c0, C)
        if spec not in specs:
            specs[spec] = len(specs)
        bid = specs[spec]
        slot = st["ring"] % 3
        st["ring"] += 1
        view = ring[slot][:, 0:nk * C].rearrange("p (k c) -> p k c", c=C)
        S.dma("pool", ring[slot][:, 0:nk * C], wblk[bid, :, 0:nk * C], writes=[("ring", slot)])
        return view, ("ring", slot)

    def next_px():
        st["px"] += 1
        return (PA, "PA") if st["px"] % 2 else (PB, "PB")

    def fm_proj(wv, wkey, nk, nj, src, skey, tcols, evac):
        nh = tcols // 2
        for j in range(nj):
            PX, pkey = next_px()
            for kc in range(nk):
                for h in range(2):
                    mm(PX[:, h, 0:nh], wv[:, kc, j * 128:(j + 1) * 128], src[:, kc, h * nh:(h + 1) * nh],
                       kc == 0, kc == nk - 1, [wkey, (skey, kc)], [pkey], signal=(kc == nk - 1 and h == 1))
            evac(j, PX[:, :, 0:nh], pkey)

    def norm_stats(src, skey, tcols, scale):
        nh = tcols // 2
        for c in range(NCH):
            sqb = sq[st["sq"] % 2]
            sk = ("sq", st["sq"] % 2)
            st["sq"] += 1
            act(sqb[:, 0:tcols], src[:, c, 0:tcols], AF.Square, [(skey, c)], [sk])
            for h in range(2):
                mm(PN[:, h, 0:nh], onesb[:], sqb[:, h * nh:(h + 1) * nh], c == 0, c == NCH - 1, [sk, "onesb"], ["PN"],
                   signal=(h == 1))
        r3 = rstd[:, 0:tcols].rearrange("p (h n) -> p h n", h=2)
        act(r3, PN[:, :, 0:nh], AF.Ln, ["PN", "epst"], ["rstd"], scale=scale, bias=epst[:, 0:1])
        act(rstd[:, 0:tcols], rstd[:, 0:tcols], AF.Exp, ["rstd"], ["rstd"], scale=-0.5)

    def prenorm(vidx, tcols):
        norm_stats(xT, "x", tcols, 1.0 / D)
        for c in range(NCH):
            stt(hn[:, c, 0:tcols], xT[:, c, 0:tcols], vec[:, vidx, c:c + 1], rstd[:, 0:tcols], ALU.mult, ALU.mult,
                [("x", c), "rstd", "vec"], [("hn", c)])

    def postnorm(vidx):
        norm_stats(Fb, "F", T, 1.0 / D)
        for c in range(NCH):
            stt(Fb[:, c, :], Fb[:, c, :], vec[:, vidx, c:c + 1], rstd[:, :], ALU.mult, ALU.mult,
                [("F", c), "rstd", "vec"], [("F", c)])
            tt(xT[:, c, :], xT[:, c, :], Fb[:, c, :], ALU.add, [("x", c), ("F", c)], [("x", c)])

    def out_proj(src, skey, nk, wname, widx, wr0, first):
        C = 4096 // nk
        nj = C // 128
        for b in range(D // C):
            wv, wkey = wload(wname, widx, wr0, nk, b * C, C)

            def evac(j, P3, pkey, b=b):
                oc = b * nj + j
                dst = Fb[:, oc, :].rearrange("p (h n) -> p h n", h=2)
                if first:
                    act(dst, P3, AF.Copy, [pkey], [("F", oc)])
                else:
                    tt(dst, dst, P3, ALU.add, [pkey, ("F", oc)], [("F", oc)])
            fm_proj(wv, wkey, nk, nj, src, skey, T, evac)

    def load_x(rows_ap, nrows, col0, L):
        r = 0
        i = 0
        while r < nrows:
            n = min(128, nrows - r)
            stg = L["stage"][i % 2]
            sk = ("stage", i % 2)
            S.dma("sp", stg[0:n, :], rows_ap[r:r + n, :], writes=[sk])
            for g in range(4):
                PX, pkey = next_px()
                for q in range(4):
                    c = g * 4 + q
                    tr(PX[:, q // 2, (q % 2) * 256:(q % 2) * 256 + n], stg[0:n, c * 128:(c + 1) * 128], identf[0:n, 0:n],
                       [sk, "identf"], [pkey], signal=(q == 3))
                src = PX[:].rearrange("p h (q n) -> p (h q) n", q=2)[:, :, 0:n]
                act(xT[:, g * 4:(g + 1) * 4, col0 + r:col0 + r + n], src, AF.Copy, [pkey],
                    [("x", c_) for c_ in range(g * 4, g * 4 + 4)])
            r += n
            i += 1

    def store_rows(srcT, skey, col0, nrows, dst_ap, L, nchunks=NCH, dcol0=0):
        r = 0
        i = st.get("stg", 0)
        while r < nrows:
            n = min(128, nrows - r)
            stg = L["stage"][i % 2]
            sk = ("stage", i % 2)
            for g in range(nchunks // 4):
                PX, pkey = next_px()
                for q in range(4):
                    c = g * 4 + q
                    tr(PX[0:n, q // 2, (q % 2) * 128:(q % 2) * 128 + 128], srcT[:, c, col0 + r:col0 + r + n], identf[:, :],
                       [(skey, c), "identf"], [pkey], signal=(q == 3))
                src = PX[0:n, :, 0:256]
                dst = stg[0:n, g * 512:(g + 1) * 512].rearrange("p (h n) -> p h n", h=2)
                act(dst, src, AF.Copy, [pkey], [sk])
            S.dma("sp", dst_ap[r:r + n, dcol0:dcol0 + nchunks * 128], stg[0:n, 0:nchunks * 128], reads=[sk])
            r += n
            i += 1
        st["stg"] = i

    def dbg_dump(name, ap, shape, reads):
        if os.environ.get("DBGDUMP"):
            d = nc.dram_tensor(name, list(shape), F32, kind="ExternalOutput").ap()
            S.dma("sp", d, ap, reads=reads)

    S.dma("sp", vec[:], vec_d[:, :, :], writes=["vec"])
    S.dma("sp", lbl[:], lbl_d[:, :, :], writes=["lbl"])
    S.dma("sp", hgn[:], hgn_d[:, :], writes=["hgn"])
    S.dma("sp", flag[:], flag_d[:, :], writes=["flag"])
    S.dma("sp", identf[:], ident_d[:, :], writes=["identf"])
    S.dma("pool", identb[:], ident_d[:, :], writes=["identb"])
    S.dma("sp", mc[:], mc_d[:, :], writes=["mc"])
    S.dma("sp", ms[:], ms_d[:, :], writes=["ms"])
    S.dma("sp", seqm[:], seqm_d[:, :], writes=["seqm"])
    S.op("dve", lambda e: e.memset(onesb[:], 1.0), [], ["onesb"])
    S.op("dve", lambda e: e.memset(epst[:], EPS), [], ["epst"])
    S.op("dve", lambda e: e.memset(Sst[:], 0.0), [], ["Sst"])
    S.op("dve", lambda e: e.memset(tailbuf[:], 0.0), [], ["tail"])
    act(lbl[:], lbl[:], AF.Exp, ["lbl"], ["lbl"])
    tt(oml[:], lbl[:, 0, :], lbl[:, 1, :], ALU.add, ["lbl"], ["oml"])
    tt(oml[:], oml[:], lbl[:, 2, :], ALU.add, ["oml", "lbl"], ["oml"])
    S.op("dve", lambda e: e.reciprocal(out=oml[:], in_=oml[:]), ["oml"], ["oml"])
    tt(lbv[:], lbl[:, 0, :], oml[:], ALU.mult, ["lbl", "oml"], ["lbv"])
    tsc(oml[:], lbv[:], -1.0, 1.0, ALU.mult, ALU.add, ["lbv"], ["oml"])
    S.barrier()

    def body():
        L = Local()
        Ld = dict(stage=[L.sb("stage0", [128, D]), L.sb("stage1", [128, D])])
        memn = L.sb("memn", [128, NCH, 256], BF16)
        checkpoint("const")
        load_x(mem, 256, 0, Ld)
        checkpoint("memload")
        norm_stats(xT, "x", 256, 1.0 / D)
        checkpoint("memnorm")
        for l in range(2):
            for c in range(NCH):
                stt(memn[:, c, :], xT[:, c, 0:256], vec[:, V_MEM + l, c:c + 1], rstd[:, 0:256], ALU.mult, ALU.mult,
                    [("x", c), "rstd", "vec"], [("memn", c)])
            checkpoint("memn%d" % l)
            for kv in range(2):
                outd = mk_o if kv == 0 else mv_o
                for b in range(8):
                    wv, wkey = wload("w_xkv", l, 0, 16, kv * D + b * 256, 256)
                    for mt in range(2):
                        PX, pkey = next_px()
                        for kc in range(NCH):
                            mm(PX[:, 0, 0:256], memn[:, kc, mt * 128:(mt + 1) * 128], wv[:, kc, :], kc == 0, kc == NCH - 1,
                               [wkey, ("memn", kc)], [pkey], signal=(kc == NCH - 1))
                        stg = Ld["stage"][mt]
                        act(stg[:, b * 256:(b + 1) * 256], PX[:, 0, 0:256], AF.Copy, [pkey], [("stage", mt)])
                        if kv == 1:
                            S.op("dve", lambda e, mt=mt, b=b, stg=stg: e.tensor_copy(out=VV[l][:, mt, b * 256:(b + 1) * 256],
                                                                                     in_=stg[:, b * 256:(b + 1) * 256]),
                                 [("stage", mt)], [("VV", l)])
                    checkpoint("tok%d%d%d" % (l, kv, b))
                    if kv == 0:
                        for j in range(2):
                            PX, pkey = next_px()
                            for kc in range(NCH):
                                mm(PX[:, 0, 0:256], wv[:, kc, j * 128:(j + 1) * 128], memn[:, kc, :], kc == 0, kc == NCH - 1,
                                   [wkey, ("memn", kc)], [pkey], signal=(kc == NCH - 1))
                            act(KT[l][:, b * 2 + j, :], PX[:, 0, 0:256], AF.Copy, [pkey], [("KT", l)])
                checkpoint("blk%d%d" % (l, kv))
                for mt in range(2):
                    S.dma("sp", outd[l, mt * 128:(mt + 1) * 128, :], Ld["stage"][mt][:, :], reads=[("stage", mt)])
                checkpoint("out%d%d" % (l, kv))
        L.free()
        checkpoint("memkv")

        def hgrn_group(h0, tcols, nchunks, seqs, Lh, prefix, first):
            Tp = nchunks * 64
            qT, W1, W2, KBb, TMP, iT, gT, onr, EL = (Lh[k] for k in ("qT", "W1", "W2", "KB", "TMP", "iT", "gT", "onr", "EL"))
            ns = len(seqs)

            def ev(dst, func, nm):
                def f(j, P3, pkey):
                    act(dst[:, j, 0:tcols].rearrange("p (h n) -> p h n", h=2), P3, func, [pkey], [(nm, j)])
                return f
            rmask = Lh["rmask"]

            def proj(nm, dst, func, blk):
                wv, wkey = wload("w_in_a", 0, 0, 16, blk * D + h0 * 128, 256)
                fm_proj(wv, wkey, 16, 2, hn, "hn", tcols, ev(dst, func, nm))

            proj("W1", W1, AF.Sigmoid, 1)
            J = range(2)
            w1 = [W1[:, j, 0:tcols] for j in J]
            w2 = [W2[:, j, 0:tcols] for j in J]
            kb = [KBb[:, j, 0:tcols] for j in J]
            tmp = [TMP[:, j, 0:tcols] for j in J]
            k1 = [("W1", j) for j in J]
            k2 = [("W2", j) for j in J]
            kk = [("KB", j) for j in J]
            kt = [("TMP", j) for j in J]
            ke = [("EL", j) for j in J]
            for j in J:
                tsc(w1[j], w1[j], oml[:, h0 + j:h0 + j + 1], lbv[:, h0 + j:h0 + j + 1], ALU.mult, ALU.add, [k1[j], "oml", "lbv"], [k1[j]])
            for j in J:
                tsc(kb[j], w1[j], -1.0, 1.0, ALU.mult, ALU.add, [k1[j]], [kk[j]])
            for j in J:
                act(w1[j], w1[j], AF.Ln, [k1[j]], [k1[j]])
            for j in J:
                S.op("dve", lambda e, j=j: e.tensor_tensor_scan(out=w2[j], data0=rmask[:, 0:tcols], data1=w1[j], initial=0.0,
                                                                op0=ALU.mult, op1=ALU.add), [k1[j], "rm"], [k2[j]])
            for j in J:
                act(tmp[j], w2[j], AF.Exp, [k2[j]], [kt[j]], scale=-1.0)
            for j in J:
                tt(kb[j], kb[j], tmp[j], ALU.mult, [kk[j], kt[j]], [kk[j]])
            for j in J:
                if Tp:
                    act(EL[:, j, 0:nchunks], w2[j][:, 63:Tp:64], AF.Exp, [k2[j]], [ke[j]])
                if ns:
                    act(EL[:, j, nchunks:nchunks + ns], w2[j][:, Tp + 7:tcols:8], AF.Exp, [k2[j]], [ke[j]])
            for j in J:
                if Tp:
                    tt(tmp[j][:, 0:Tp].rearrange("p (c i) -> p c i", i=64), kb[j][:, 0:Tp].rearrange("p (c i) -> p c i", i=64),
                       EL[:, j, 0:nchunks].unsqueeze(2).broadcast_to([128, nchunks, 64]), ALU.mult, [kk[j], ke[j]], [kt[j]])
                if ns:
                    tt(tmp[j][:, Tp:tcols].rearrange("p (c i) -> p c i", i=8), kb[j][:, Tp:tcols].rearrange("p (c i) -> p c i", i=8),
                       EL[:, j, nchunks:nchunks + ns].unsqueeze(2).broadcast_to([128, ns, 8]), ALU.mult, [kk[j], ke[j]], [kt[j]])
            proj("iT", iT, AF.Copy, 2)
            if not prefix:
                proj("qT", qT, AF.Silu, 0)
                for j in J:
                    act(w1[j], w2[j], AF.Exp, [k2[j]], [k1[j]])
                for j in J:
                    tt(qT[:, j, 0:tcols], qT[:, j, 0:tcols], w1[j], ALU.mult, [("qT", j), k1[j]], [("qT", j)])
                proj("gT", gT, AF.Silu, 3)
            if not prefix:
                checkpoint("hg_chain")
            chunks = [(c * 64, 64, "p", None, c) for c in range(nchunks)]
            for s0 in range(0, ns, 4):
                chunks.append((Tp + s0 * 8, 32, "s", seqs[s0:s0 + 4], nchunks + s0))
            NB = 3
            nck = len(chunks)
            SbL = Lh["SbL"]
            if not prefix:
                act(SbL[0][:], Sst[:, h0:h0 + 2, :], AF.Copy, [("Sst", h0), ("Sst", h0 + 1)], [("SbL", 0)])

            def bufs(ci):
                b2 = ci % NB
                return Lh["Vtok"][b2], Lh["Ktok"][b2], ("Vtok", b2), ("Ktok", b2), b2

            def stT(ci):
                c0, n, kind, sq_, eli = chunks[ci]
                Vtok, Ktok, kV, kK, b2 = bufs(ci)
                for j in range(2):
                    tr(PT[0:n, j * 128:(j + 1) * 128], iT[:, j, c0:c0 + n], identb[:, :], [("iT", j), "identb"], ["PT"], signal=False)
                    tr(PT[0:n, 256 + j * 128:256 + (j + 1) * 128], TMP[:, j, c0:c0 + n], identb[:, :], [("TMP", j), "identb"], ["PT"],
                       signal=(j == 1))
                act(Vtok[0:n, :], PT[0:n, 0:256], AF.Copy, ["PT"], [kV])
                act(Ktok[0:n, :], PT[0:n, 256:512], AF.Copy, ["PT"], [kK])

            def stA(ci):
                c0, n, kind, sq_, eli = chunks[ci]
                b2 = ci % NB
                AT = Lh["AT"][b2]
                for j in range(2):
                    mm(PN[0:n, 0, j * 64:j * 64 + n], KBb[:, j, c0:c0 + n], qT[:, j, c0:c0 + n], True, True,
                       [("KB", j), ("qT", j)], ["PN"], signal=(j == 1))
                msk = mc if kind == "p" else ms
                tt(AT[0:n, :, 0:n], PN[0:n, 0, 0:128].rearrange("p (j t) -> p j t", j=2)[:, :, 0:n],
                   msk[0:n, 0:n].unsqueeze(1).broadcast_to([n, 2, n]), ALU.mult, ["PN", "mc", "ms"], [("AT", b2)])

            def stK(ci):
                c0, n, kind, sq_, eli = chunks[ci]
                if kind != "p":
                    return
                Vtok, Ktok, kV, kK, b2 = bufs(ci)
                PK, kPK = (PG[:, 0:256], "PG") if ci % 2 == 0 else (PB[:, 0, 0:256], "PB")
                for j in range(2):
                    mm(PK[:, j * 128:(j + 1) * 128], Ktok[0:n, j * 128:(j + 1) * 128], Vtok[0:n, j * 128:(j + 1) * 128], True, True,
                       [kK, kV], [kPK], signal=(j == 1))
                nxt = (ci + 1) % 3
                for j in range(2):
                    stt(Sst[:, h0 + j, :], Sst[:, h0 + j, :], EL[:, j, eli:eli + 1], PK[:, j * 128:(j + 1) * 128], ALU.mult, ALU.add,
                        [("Sst", h0 + j), ("EL", j), kPK], [("Sst", h0 + j)])
                if not prefix:
                    act(SbL[nxt][:], Sst[:, h0:h0 + 2, :], AF.Copy, [("Sst", h0), ("Sst", h0 + 1)], [("SbL", nxt)])

            def stO(ci):
                c0, n, kind, sq_, eli = chunks[ci]
                Vtok, Ktok, kV, kK, b2 = bufs(ci)
                AT = Lh["AT"][b2]
                PO, kPO = (PN[:, 1, 0:128], "PN1") if ci % 2 == 0 else (PA[:, 0, 0:128], "PA")
                cur = ci % 3
                if kind == "s":
                    S.op("dve", lambda e: e.memset(PO, 0.0), [], [kPO])
                for j in range(2):
                    mm(PO[:, j * 64:j * 64 + n], Vtok[0:n, j * 128:(j + 1) * 128], AT[0:n, j, 0:n], kind == "p", False,
                       [kV, ("AT", b2)], [kPO], signal=False)
                    if kind == "p":
                        mm(PO[:, j * 64:j * 64 + n], SbL[cur][:, j, :], qT[:, j, c0:c0 + n], False, True,
                           [("SbL", cur), ("qT", j)], [kPO], signal=(j == 1))
                if kind == "s":
                    for qi, s in enumerate(sq_):
                        act(Lh["Ssb"][qi][:], Lh["Ssf"][qi][:], AF.Copy, [("Ssf", qi)], [("Ssb", qi)])
                    for qi, s in enumerate(sq_):
                        S.op("dve", lambda e, qi=qi, Ktok=Ktok, n=n: e.tensor_scalar(
                            out=sq[qi // 2][0:n, (qi % 2) * 256:(qi % 2) * 256 + 256], in0=Ktok[0:n, :],
                            scalar1=seqm[0:n, qi:qi + 1], scalar2=None, op0=ALU.mult),
                            [kK, "seqm"], [("sq", qi // 2)])
                    for qi, s in enumerate(sq_):
                        b3 = qi
                        Ssf, Ssb = Lh["Ssf"][b3], Lh["Ssb"][b3]
                        for j in range(2):
                            mm(PO[:, j * 64 + qi * 8:j * 64 + qi * 8 + 8], Ssb[:, j, :], qT[:, j, c0 + qi * 8:c0 + qi * 8 + 8], False, True,
                               [("Ssb", b3), ("qT", j)], [kPO], signal=(j == 1))
                        Ktm = sq[qi // 2][0:n, (qi % 2) * 256:(qi % 2) * 256 + 256]
                        for j in range(2):
                            mm(PG[:, j * 128:(j + 1) * 128], Ktm[:, j * 128:(j + 1) * 128], Vtok[0:n, j * 128:(j + 1) * 128], True, True,
                               [("sq", qi // 2), kV], ["PG"], signal=(j == 1))
                        So = Lh["Sout"][qi % 2]
                        for j in range(2):
                            stt(So[:, j, :], Ssf[:, j, :], EL[:, j, eli + qi:eli + qi + 1], PG[:, j * 128:(j + 1) * 128], ALU.mult, ALU.add,
                                [("Ssf", b3), ("EL", j), "PG"], [("Sout", qi % 2)])
                        S.dma("sp", S_s[s, h0:h0 + 2, :, :].rearrange("h k v -> k h v"), So[:], reads=[("Sout", qi % 2)])
                        if ci + 1 < nck and qi < len(chunks[ci + 1][3]):
                            load_state(chunks[ci + 1][3][qi], qi)
                o3 = PO.rearrange("p (j t) -> p j t", j=2)[:, :, 0:n]
                act(W1[:, :, c0:c0 + n], o3, AF.Copy, [kPO], [("W1", 0), ("W1", 1)])

            def load_state(s, qi):
                S.dma("sp", Lh["Ssf"][qi][:], sh[s, h0:h0 + 2, :, :].rearrange("h k v -> k h v"), writes=[("Ssf", qi)])

            if not prefix and nck > nchunks:
                for qi, s in enumerate(chunks[nchunks][3]):
                    load_state(s, qi)
            for c_ in range(min(2, nck)):
                stT(c_)
                if not prefix:
                    stA(c_)
            stK(0)
            for ci in range(nck):
                if ci + 2 < nck:
                    stT(ci + 2)
                    if not prefix:
                        stA(ci + 2)
                if ci + 1 < nck:
                    stK(ci + 1)
                if not prefix:
                    stO(ci)
            if not prefix:
                nh = tcols // 2
                for j in range(2):
                    act(TMP[:, j, 0:tcols], W1[:, j, 0:tcols], AF.Square, [("W1", j)], [("TMP", j)])
                    PX, pkey = next_px()
                    for h in range(2):
                        mm(PX[:, h, 0:nh], onesb[:], TMP[:, j, h * nh:(h + 1) * nh], True, True, [("TMP", j), "onesb"], [pkey],
                           signal=(h == 1))
                    act(W2[:, j, 0:tcols].rearrange("p (h n) -> p h n", h=2), PX[:, :, 0:nh], AF.Ln, [pkey, "epst"], [("W2", j)],
                        scale=1.0 / 128, bias=epst[:, 0:1])
                    act(W2[:, j, 0:tcols], W2[:, j, 0:tcols], AF.Exp, [("W2", j)], [("W2", j)], scale=-0.5)
                    tt(W1[:, j, 0:tcols], W1[:, j, 0:tcols], W2[:, j, 0:tcols], ALU.mult, [("W1", j), ("W2", j)], [("W1", j)])
                    stt(onr[:, j, 0:tcols], W1[:, j, 0:tcols], hgn[:, 0:1], gT[:, j, 0:tcols], ALU.mult, ALU.mult,
                        [("W1", j), "hgn", ("gT", j)], [("onr", j)])
            if not prefix:
                checkpoint("hg_chunks")
                out_proj(onr, "onr", 2, "w_out_a", 0, h0 * 128, first)
                checkpoint("hg_out")

        def hgrn_locals(Lx):
            d = {}
            for nm, dt in (("qT", BF16), ("W1", F32), ("W2", F32), ("KB", BF16), ("TMP", BF16), ("iT", BF16), ("gT", BF16), ("onr", BF16)):
                d[nm] = Lx.sb("h_" + nm, [128, 2, T], dt)
            d["EL"] = Lx.sb("h_EL", [128, 2, 24])
            d["Vtok"] = [Lx.sb("h_Vtok%d" % i, [64, 256], BF16) for i in range(3)]
            d["Ktok"] = [Lx.sb("h_Ktok%d" % i, [64, 256], BF16) for i in range(3)]
            d["Ktm"] = Lx.sb("h_Ktm", [64, 256], BF16)
            d["AT"] = [Lx.sb("h_AT%d" % i, [64, 2, 64], BF16) for i in range(3)]
            d["SbL"] = [Lx.sb("h_SbL%d" % i, [128, 2, 128], BF16) for i in range(3)]
            d["Ssf"] = [Lx.sb("h_Ssf%d" % i, [128, 2, 128]) for i in range(4)]
            d["Ssb"] = [Lx.sb("h_Ssb%d" % i, [128, 2, 128], BF16) for i in range(4)]
            d["Sout"] = [Lx.sb("h_Sout%d" % i, [128, 2, 128]) for i in range(2)]
            d["sloaded"] = {}
            return d

        for sbk in range(NPRE // TPRE):
            L = Local()
            Ld = dict(stage=[L.sb("stage0", [128, D]), L.sb("stage1", [128, D])])
            load_x(xpre[sbk * TPRE:(sbk + 1) * TPRE, :], TPRE, 0, Ld)
            prenorm(V_MIXPRE + 0, TPRE)
            L.free()
            L = Local()
            Lh = hgrn_locals(L)
            Lh["rmask"] = rm
            S.dma("sp", rm[:, 0:TPRE], rmpre_d[:, :], writes=["rm"])
            for g in range(8):
                hgrn_group(2 * g, TPRE, TPRE // 64, [], Lh, True, False)
            L.free()

        checkpoint("prefix")

        def attn_layer(l, p):
            nch, seqs = p["nch"], p["seqs"]
            Tp = nch * 64
            nhp = Tp // 2
            ns = len(seqs)
            L = Local()
            qh = L.sb("a_qh", [128, 4, T], BF16)
            ah = L.sb("a_ah", [128, 4, T], BF16)
            ES = L.sb("a_ES", [128, 2, 576], BF16)
            rinv = L.sb("a_rinv", [128, 576])
            Ks = [L.sb("a_Ks%d" % i, [128, 2, 512], BF16) for i in range(2)]
            Vs = [L.sb("a_Vs%d" % i, [128, 2, 512], BF16) for i in range(3)]
            KTs = [L.sb("a_KTs%d" % i, [128, 4, 256], BF16) for i in range(2)]
            ESs = [L.sb("a_ESs%d" % i, [128, 2, 8], BF16) for i in range(2)]
            rinvs = L.sb("a_rinvs", [128, 8])
            sc = 512.0 ** -0.5
            for hd in range(4):
                for b in range(2):
                    wv, wkey = wload("w_xq", l, 0, 16, hd * 512 + b * 256, 256)

                    def evq(j, P3, pkey, b=b):
                        act(qh[:, 2 * b + j, :].rearrange("p (h n) -> p h n", h=2), P3, AF.Copy, [pkey], [("qh", 2 * b + j)])
                    fm_proj(wv, wkey, 16, 2, hn, "hn", T, evq)
                for mt in range(2):
                    PX, pkey = next_px()
                    for dc in range(4):
                        for h in range(2):
                            mm(PX[:, h, 0:nhp], KT[l][:, hd * 4 + dc, mt * 128:(mt + 1) * 128], qh[:, dc, h * nhp:(h + 1) * nhp],
                               dc == 0, dc == 3, [("KT", l), ("qh", dc)], [pkey], signal=(dc == 3 and h == 1))
                    act(ES[:, mt, 0:Tp].rearrange("p (h n) -> p h n", h=2), PX[:, :, 0:nhp], AF.Exp, [pkey], [("ES", mt)], scale=sc)
                for mt in range(2):
                    for h in range(2):
                        mm(PN[:, h, 0:nhp], onesb[:], ES[:, mt, h * nhp:(h + 1) * nhp], mt == 0, mt == 1, [("ES", mt), "onesb"], ["PN"],
                           signal=(mt == 1 and h == 1))
                S.op("dve", lambda e: e.reciprocal(out=rinv[:, 0:Tp].rearrange("p (h n) -> p h n", h=2), in_=PN[:, :, 0:nhp]),
                     ["PN"], ["rinv"])
                for dc in range(4):
                    PX, pkey = next_px()
                    for mt in range(2):
                        for h in range(2):
                            mm(PX[:, h, 0:nhp], VV[l][:, mt, hd * 512 + dc * 128: hd * 512 + (dc + 1) * 128], ES[:, mt, h * nhp:(h + 1) * nhp],
                               mt == 0, mt == 1, [("VV", l), ("ES", mt)], [pkey], signal=(mt == 1 and h == 1))
                    tt(ah[:, dc, 0:Tp].rearrange("p (h n) -> p h n", h=2), PX[:, :, 0:nhp],
                       rinv[:, 0:Tp].rearrange("p (h n) -> p h n", h=2), ALU.mult, [pkey, "rinv"], [("ah", dc)])
                def stA1(si, hd=hd):
                    s = seqs[si]
                    b2 = si % 2
                    b3 = si % 3
                    S.dma("pool", Ks[b2][:], ck[l, s, :, hd * 512:(hd + 1) * 512].rearrange("(t p) d -> p t d", p=128), writes=[("Ks", b2)])
                    S.dma("pool", Vs[b3][:], cv[l, s, :, hd * 512:(hd + 1) * 512].rearrange("(t p) d -> p t d", p=128), writes=[("Vs", b3)])
                    for dc in range(4):
                        for mt in range(2):
                            tr(PT[:, dc * 256 + mt * 128: dc * 256 + (mt + 1) * 128], Ks[b2][:, mt, dc * 128:(dc + 1) * 128], identb[:, :],
                               [("Ks", b2), "identb"], ["PT"], signal=(dc == 3 and mt == 1))
                    act(KTs[b2][:].rearrange("p c m -> p (c m)"), PT[:, :], AF.Copy, ["PT"], [("KTs", b2)])

                def stA2(si, hd=hd):
                    b2 = si % 2
                    c0 = Tp + si * 8
                    for mt in range(2):
                        for dc in range(4):
                            mm(PG[:, mt * 8:(mt + 1) * 8], KTs[b2][:, dc, mt * 128:(mt + 1) * 128], qh[:, dc, c0:c0 + 8], dc == 0, dc == 3,
                               [("KTs", b2), ("qh", dc)], ["PG"], signal=(dc == 3 and mt == 1))
                    act(ESs[b2][:].rearrange("p m t -> p (m t)"), PG[:, 0:16], AF.Exp, ["PG"], [("ESs", b2)], scale=sc)

                def stB(si, hd=hd):
                    b2 = si % 2
                    b3 = si % 3
                    c0 = Tp + si * 8
                    for mt in range(2):
                        mm(PN[:, 0, 0:8], onesb[:], ESs[b2][:, mt, :], mt == 0, mt == 1, [("ESs", b2), "onesb"], ["PN"], signal=(mt == 1))
                    S.op("dve", lambda e: e.reciprocal(out=rinvs[:], in_=PN[:, 0, 0:8]), ["PN"], ["rinvs"])
                    for dc in range(4):
                        for mt in range(2):
                            mm(PN[:, 1, dc * 8:(dc + 1) * 8], Vs[b3][:, mt, dc * 128:(dc + 1) * 128], ESs[b2][:, mt, :], mt == 0, mt == 1,
                               [("Vs", b3), ("ESs", b2)], ["PN"], signal=(dc == 3 and mt == 1))
                    tt(ah[:, :, c0:c0 + 8], PN[:, 1, 0:32].rearrange("p (c t) -> p c t", t=8),
                       rinvs[:].unsqueeze(1).broadcast_to([128, 4, 8]), ALU.mult, ["PN", "rinvs"], [("ah", dc_) for dc_ in range(4)])

                stA1(0)
                stA2(0)
                for si in range(ns):
                    if si + 1 < ns:
                        stA1(si + 1)
                    stB(si)
                    if si + 1 < ns:
                        stA2(si + 1)
                out_proj(ah, "ah", 4, "w_xo", l, hd * 512, hd == 0)
            L.free()

        def mlp_layer(l):
            L = Local()
            hT = L.sb("m_hT", [128, 8, T], BF16)
            rr = [L.sb("m_r%d" % i, [128, T], BF16) for i in range(2)]
            for g in range(8):
                for b in range(4):
                    wv, wkey = wload("w_up", l, 0, 16, g * 1024 + b * 256, 256)

                    def evu(j, P3, pkey, b=b):
                        i2 = st.get("rr", 0) % 2
                        st["rr"] = st.get("rr", 0) + 1
                        r3 = rr[i2][:, :].rearrange("p (h n) -> p h n", h=2)
                        act(r3, P3, AF.Relu, [pkey], [("rr", i2)])
                        tt(hT[:, 2 * b + j, :], rr[i2][:, :], rr[i2][:, :], ALU.mult, [("rr", i2)], [("hT", 2 * b + j)])
                    fm_proj(wv, wkey, 16, 2, hn, "hn", T, evu)
                out_proj(hT, "hT", 8, "w_down", l, g * 1024, g == 0)
            L.free()

        def pool_layer(p, pi):
            nch, seqs = p["nch"], p["seqs"]
            Tp = nch * 64
            ns = len(seqs)
            Ts = ns * 8
            LE = 15 + Tp + 23 * ns
            L = Local()
            Ec = [L.sb("p_Ec%d" % i, [128, 803]) for i in range(2)]
            Wa = L.sb("p_Wa", [128, 803])
            Wb = L.sb("p_Wb", [128, 803])
            pl = L.sb("p_pl", [128, 4, T], BF16)
            mx = L.sb("p_mx", [128, 4, T], BF16)
            invc = L.sb("p_invc", [128, T])
            ust = L.sb("p_ust", [128, 4, 96])
            oldst = L.sb("p_oldst", [128, 512])
            Ld = dict(stage=[L.sb("p_stg0", [128, 512]), L.sb("p_stg1", [128, 512])])
            for gi in range(4):
                w = 2 << gi
                S.dma("sp", invc[:], invc_d[pi, gi, :, :], writes=["invc"])
                for b in range(2):
                    wv, wkey = wload("w_in_b", 0, 0, 16, gi * 512 + b * 256, 256)

                    def evp(j, P3, pkey, b=b, gi=gi, w=w):
                        cg = 2 * b + j
                        c = gi * 4 + cg
                        i2 = st.get("ec", 0) % 2
                        st["ec"] = st.get("ec", 0) + 1
                        E = Ec[i2]
                        ke = ("Ec", i2)
                        act(Wb[:, 0:T].rearrange("p (h n) -> p h n", h=2), P3, AF.Copy, [pkey], ["Wb"])
                        S.op("dve", lambda e: e.tensor_copy(out=E[:, 15:15 + Tp], in_=Wb[:, 0:Tp]), ["Wb"], [ke])
                        S.op("dve", lambda e: e.tensor_copy(out=E[:, 15 + Tp:LE].rearrange("p (s r) -> p s r", r=23)[:, :, 15:23],
                                                            in_=Wb[:, Tp:T].rearrange("p (s r) -> p s r", r=8)), ["Wb"], [ke])
                        S.op("dve", lambda e: e.tensor_copy(out=ust[:, cg, 0:Ts], in_=Wb[:, Tp:T]), ["Wb"], [("ust", cg)])
                        act(E[:, 0:15], tailbuf[:, c, :], AF.Copy, [("tail", c)], [ke])
                        if pi == 0:
                            S.op("dve", lambda e: e.tensor_scalar(out=E[:, 15:79], in0=E[:, 15:79], scalar1=flag[:, 0:1],
                                                                  scalar2=None, op0=ALU.mult), [ke, "flag"], [ke])
                        act(tailbuf[:, c, :], E[:, Tp:Tp + 15], AF.Copy, [ke], [("tail", c)])
                        for s0 in range(0, ns, 8):
                            n8 = min(8, ns - s0)
                            if cg == 0:
                                pass
                            PXo, pko = next_px()
                            S.dma("sp", oldst[0:n8 * 15, 0:128],
                                  spool[seqs[s0]:seqs[s0] + n8, :, c * 128:(c + 1) * 128].rearrange("s r d -> (s r) d"),
                                  writes=["oldst"])
                            tr(PXo[:, 0, 0:n8 * 15], oldst[0:n8 * 15, 0:128], identf[0:n8 * 15, 0:n8 * 15], ["oldst", "identf"], [pko])
                            act(E[:, 15 + Tp + 23 * s0:15 + Tp + 23 * (s0 + n8)].rearrange("p (s r) -> p s r", r=23)[:, :, 0:15],
                                PXo[:, 0, 0:n8 * 15].rearrange("p (s r) -> p s r", r=15), AF.Copy, [pko], [ke])
                        cur = E
                        ck_ = ke
                        bufs = [(Wa, "Wa"), (Wb, "Wb")]
                        lo = 0
                        for k in range(gi + 1):
                            sh_ = 1 << k
                            nb, nk_ = bufs[k % 2]
                            lo2 = lo + sh_
                            tt(nb[:, lo2:LE], cur[:, lo2:LE], cur[:, lo2 - sh_:LE - sh_], ALU.add, [ck_], [nk_])
                            cur, ck_, lo = nb, nk_, lo2
                        oth, ko = bufs[(gi + 1) % 2]
                        tt(oth[:, 0:Tp], cur[:, 15:15 + Tp], invc[:, 0:Tp], ALU.mult, [ck_, "invc"], [ko])
                        tt(pl[:, cg, 0:Tp], oth[:, 0:Tp], E[:, 15:15 + Tp], ALU.subtract, [ko, ke], [("pl", cg)])
                        cs3 = cur[:, 15 + Tp:LE].rearrange("p (s r) -> p s r", r=23)[:, :, 15:23]
                        es3 = E[:, 15 + Tp:LE].rearrange("p (s r) -> p s r", r=23)[:, :, 15:23]
                        o3 = oth[:, Tp:T].rearrange("p (s r) -> p s r", r=8)
                        tt(o3, cs3, invc[:, Tp:T].rearrange("p (s r) -> p s r", r=8), ALU.mult, [ck_, "invc"], [ko])
                        tt(pl[:, cg, Tp:T].rearrange("p (s r) -> p s r", r=8), o3, es3, ALU.subtract, [ko, ke], [("pl", cg)])
                    fm_proj(wv, wkey, 16, 2, hn, "hn", T, evp)
                for s0 in range(0, ns, 12):
                    n12 = min(12, ns - s0)
                    PXo, pko = next_px()
                    for cg in range(4):
                        tr(PXo[0:n12 * 8, cg // 2, (cg % 2) * 128:(cg % 2) * 128 + 128], ust[:, cg, s0 * 8:(s0 + n12) * 8], identf[:, :],
                           [("ust", cg), "identf"], [pko], signal=(cg == 3))
                    stg = Ld["stage"][gi % 2]
                    sk = ("pstg", gi % 2)
                    act(stg[0:n12 * 8, :].rearrange("p (h n) -> p h n", h=2), PXo[0:n12 * 8, :, 0:256], AF.Copy, [pko], [sk])
                    for si in range(n12):
                        s = seqs[s0 + si]
                        S.dma("sp", pool_s[s, 7:15, gi * 512:(gi + 1) * 512], stg[si * 8:(si + 1) * 8, :], reads=[sk])
                for b in range(2):
                    wv, wkey = wload("pool_w", gi, 0, 4, b * 256, 256)

                    def evm(j, P3, pkey, b=b, gi=gi):
                        cg = 2 * b + j
                        act(mx[:, cg, :].rearrange("p (h n) -> p h n", h=2), P3, AF.Copy, [pkey], [("mx", cg)],
                            scale=vec[:, V_PSCALE, gi * 4 + cg:gi * 4 + cg + 1])
                    fm_proj(wv, wkey, 4, 2, pl, "pl", T, evm)
                out_proj(mx, "mx", 4, "w_out_b", 0, gi * 512, gi == 0)
            L.free()

        S.dma("sp", pool_s[:, 0:7, :], spool[:, 8:15, :])

        for pi, p in enumerate(PASSES):
            nch, seqs = p["nch"], p["seqs"]
            Tp = nch * 64
            ns = len(seqs)
            L = Local()
            Ld = dict(stage=[L.sb("stage0", [128, D]), L.sb("stage1", [128, D])])
            load_x(xp[p["row0"]:p["row0"] + Tp, :], Tp, 0, Ld)
            load_x(xs[seqs[0] * 8:(seqs[0] + ns) * 8, :], ns * 8, Tp, Ld)
            S.dma("sp", rm[:, :], rm_d[pi, :, :], writes=["rm"])
            L.free()
            checkpoint("p%dload" % pi)
            for l in range(2):
                prenorm(V_MIXPRE + l, T)
                if l == 0:
                    L = Local()
                    Lh = hgrn_locals(L)
                    Lh["rmask"] = rm
                    for g in range(8):
                        hgrn_group(2 * g, T, nch, seqs, Lh, False, g == 0)
                    L.free()
                else:
                    pool_layer(p, pi)
                if pi == 0 and l == 0:
                    dbg_dump("d_hg", Fb[:, :, Tp:Tp + 32], [128, 16, 32], [("F", c_) for c_ in range(16)])
                checkpoint("p%dl%dmix0" % (pi, l))
                postnorm(V_MIXPOST + l)
                checkpoint("p%dl%dmix" % (pi, l))
                prenorm(V_XPRE + l, T)
                attn_layer(l, p)
                if pi == 0 and l == 0:
                    dbg_dump("d_at", Fb[:, :, Tp:Tp + 32], [128, 16, 32], [("F", c_) for c_ in range(16)])
                postnorm(V_XPOST + l)
                checkpoint("p%dl%dattn" % (pi, l))
                prenorm(V_MLPPRE + l, T)
                mlp_layer(l)
                postnorm(V_MLPPOST + l)
                checkpoint("p%dl%dmlp" % (pi, l))
            L = Local()
            Ld = dict(stage=[L.sb("stage0", [128, D]), L.sb("stage1", [128, D])])
            if pi == 0:
                store_rows(xT, "x", 64, Tp - 64, y_p[0:Tp - 64, :], Ld)
            else:
                store_rows(xT, "x", 0, Tp, y_p[512:1024, :], Ld)
            store_rows(xT, "x", Tp, ns * 8, y_s[seqs[0] * 8:(seqs[0] + ns) * 8, :], Ld)
            L.free()

        for h in range(16):
            S.dma("sp", S_p[h, :, :], Sst[:, h, :], reads=[("Sst", h)])
        L = Local()
        Ld = dict(stage=[L.sb("stage0", [128, D]), L.sb("stage1", [128, D])])
        store_rows(tailbuf, "tail", 0, 15, pool_p[:, :], Ld)
        L.free()
    try:
        body()
    except _Stop:
        for Lx in list(reversed(live)):
            Lx.free()
    S.barrier(final=True)
    S.close()
    for cm in reversed(cms):
        cm.__exit__(None, None, None)
    S.specs = specs
    return nc, S


_CACHE = {}


def kernel(x_prompt, x_sample, state_hgrn, state_pool, cache_mem_k, cache_mem_v, mem_prompt,
           w_in_a, hg_lb_logits, hg_norm, w_out_a, w_in_b, pool_w, pool_scale, w_out_b,
           norm_mem, w_xq, w_xkv, w_xo, norm_mix_pre, norm_mix_post, norm_x_pre, norm_x_post,
           norm_mlp_pre, norm_mlp_post, w_up, w_down):
    f = np.float32
    A = lambda a: np.ascontiguousarray(np.asarray(a, dtype=f))
    x_prompt, x_sample = A(x_prompt), A(x_sample)
    if "nc" not in _CACHE:
        _CACHE["nc"], S_ = build_program()
        _CACHE["specs"] = S_.specs
    nc = _CACHE["nc"]
    vecs = [norm_mix_pre, norm_mix_post, norm_x_pre, norm_x_post, norm_mlp_pre, norm_mlp_post, norm_mem]
    rows = []
    for v in vecs:
        v = A(v)
        rows += [v[0], v[1]]
    rows.append(A(pool_scale)[0])
    vec = np.ascontiguousarray(np.stack(rows, 0).reshape(15, 16, 128).transpose(2, 0, 1))
    lbl = np.ascontiguousarray(A(hg_lb_logits).reshape(3, 16, 128).transpose(2, 0, 1))
    hgn = np.ascontiguousarray(A(hg_norm)[0].reshape(128, 1))
    ident = np.eye(128, dtype=f)
    mc = np.triu(np.ones((64, 64), f))
    sid = np.arange(32) // 8
    ms = (np.triu(np.ones((32, 32), f)) * (sid[:, None] == sid[None, :])).astype(f)
    seqm = (sid[:, None] == np.arange(4)[None, :]).astype(f)
    rm = np.ones((2, 128, T), f)
    invc = np.ones((2, 4, 128, T), f)
    rmpre = np.ones((128, TPRE), f)
    rmpre[:, 0::64] = 0.0
    wsrc = dict(w_in_a=A(w_in_a), w_out_a=A(w_out_a), w_in_b=A(w_in_b), pool_w=A(pool_w)[0],
                w_out_b=A(w_out_b), w_xq=A(w_xq), w_xkv=A(w_xkv), w_xo=A(w_xo), w_up=A(w_up), w_down=A(w_down))
    specs = _CACHE["specs"]
    assert len(specs) <= NBLK, len(specs)
    wblk = np.zeros((NBLK, 128, 4096), f)
    for (name, idx, r0, nk, c0, C), bid in specs.items():
        blk = wsrc[name][idx][r0:r0 + nk * 128, c0:c0 + C]
        wblk[bid, :, 0:nk * C] = blk.reshape(nk, 128, C).transpose(1, 0, 2).reshape(128, nk * C)
    weights = dict(wblk=wblk)
    state_hgrn, state_pool = A(state_hgrn), A(state_pool)
    cache_mem_k, cache_mem_v, mem_prompt = A(cache_mem_k), A(cache_mem_v), A(mem_prompt)
    in_maps = []
    for c in range(8):
        b, half = c // 2, c % 2
        rmc = rm.copy()
        ivc = invc.copy()
        for pi, p in enumerate(PASSES):
            Tp = p["nch"] * 64
            rmc[pi, :, 0:Tp:64] = 0.0
            rmc[pi, :, Tp::8] = 0.0
            pos = half * 1024 - 64 + p["row0"] + np.arange(Tp)
            for gi in range(4):
                w = 2 << gi
                ivc[pi, gi, :, 0:Tp] = (1.0 / np.minimum(w, np.maximum(pos, 0) + 1))[None, :]
                ivc[pi, gi, :, Tp:] = 1.0 / w
        xpc = np.zeros((1088, 2048), f)
        lo = half * 1024 - 64
        if lo < 0:
            xpc[64:] = x_prompt[b, 0:1024]
        else:
            xpc[:] = x_prompt[b, lo:lo + 1088]
        xpre = np.zeros((NPRE, 2048), f)
        if half == 1:
            xpre[:] = x_prompt[b, 0:NPRE]
        s0 = c * 16
        m = dict(
            xp=xpc, xpre=xpre, xs=np.ascontiguousarray(x_sample[s0:s0 + 16].reshape(128, 2048)),
            sh=np.ascontiguousarray(state_hgrn[0, s0:s0 + 16]), spool=np.ascontiguousarray(state_pool[0, s0:s0 + 16]),
            ck=np.ascontiguousarray(cache_mem_k[:, s0:s0 + 16].reshape(2, 16, 256, 2048)),
            cv=np.ascontiguousarray(cache_mem_v[:, s0:s0 + 16].reshape(2, 16, 256, 2048)),
            mem=np.ascontiguousarray(mem_prompt[b]), vec=vec, lbl=lbl, hgn=hgn,
            flag=np.full((128, 1), float(half), f), ident=ident, mc=mc, ms=ms, seqm=seqm, rm=rmc, rmpre=rmpre, invc=ivc,
        )
        m.update(weights)
        in_maps.append(m)
    if _CACHE.get("dbg_only_maps"):
        return in_maps
    res = run_bass_kernel_spmd(nc, in_maps, core_ids=list(range(8)))
    R = res.results
    y_prompt = np.stack([np.concatenate([R[2 * b]["y_p"], R[2 * b + 1]["y_p"]], 0) for b in range(4)], 0)
    y_sample = np.concatenate([R[c]["y_s"] for c in range(8)], 0).reshape(128, 8, 2048)
    st_h_p = np.stack([R[2 * b + 1]["S_p"] for b in range(4)], 0)[None]
    st_p_p = np.stack([R[2 * b + 1]["pool_p"] for b in range(4)], 0)[None]
    mk = np.stack([R[2 * b]["mk_o"] for b in range(4)], 1).reshape(2, 4, 256, 4, 512)
    mv = np.stack([R[2 * b]["mv_o"] for b in range(4)], 1).reshape(2, 4, 256, 4, 512)
    st_h_s = np.concatenate([R[c]["S_s"] for c in range(8)], 0)[None]
    st_p_s = np.concatenate([R[c]["pool_s"] for c in range(8)], 0)[None]
    return (y_prompt.astype(f), y_sample.astype(f), st_h_p.astype(f), st_p_p.astype(f), mk.astype(f), mv.astype(f),
            st_h_s.astype(f), st_p_s.astype(f))
```
